# Optimizing a Trainium2 kernel written in Bass

```python
import math
import jax, jax.numpy as jnp
from jax import lax
import numpy as np

D_MODEL = 1024
BATCH = 32
SEQ = 256
DEPTH = 4
DEC_BATCH = 4
DEC_SEQ = 4096
PAST_LEN = 512

GRID_W = 64
POS_BASE = 10000.0
EPS = 1e-6
N_MOD = 9
D_FF = 2816
N_HEADS_A = 16
HEAD_DIM_A = 64
D_INNER_A = N_HEADS_A * HEAD_DIM_A
N_GROUPS_A = 2
D_STATE = 128
CONV_A = 5
SSD_CHUNK = 128
D_XBC = D_INNER_A + 2 * N_GROUPS_A * D_STATE
POOL_WINDOWS = (2, 4, 8, 16)
N_POOL_GROUPS = len(POOL_WINDOWS)
POOL_GROUP_DIM = 128
D_POOL = N_POOL_GROUPS * POOL_GROUP_DIM
D_IN_EVEN = D_INNER_A + D_XBC + 2 * N_HEADS_A + D_POOL
D_OUT_EVEN = D_INNER_A + D_POOL
N_HEADS_C = 8
MLP_CHUNK = 128
D_C = 1024
HEAD_DIM_C = D_C // N_HEADS_C
D_D = 1024
CONV_D = 31
D_IN_ODD = 2 * D_C + 2 * D_D
D_OUT_ODD = D_C + D_D
N_EVEN = (DEPTH + 1) // 2
N_ODD = DEPTH // 2

kernel_name = 'hybrid_ssd_pool_gmlp_conformer_diffusion_step'


def rms_norm(x, g):
    xf = x.astype(jnp.float32)
    y = xf * lax.rsqrt(jnp.mean(xf * xf, axis=-1, keepdims=True) + EPS)
    return (y * g.astype(jnp.float32)).astype(x.dtype)


def layer_norm(x, g, b):
    xf = x.astype(jnp.float32)
    xc = xf - jnp.mean(xf, axis=-1, keepdims=True)
    y = xc * lax.rsqrt(jnp.mean(xc * xc, axis=-1, keepdims=True) + EPS)
    return (y * g.astype(jnp.float32) + b.astype(jnp.float32)).astype(x.dtype)


def modulate(h, shift, scale):
    return h * (1.0 + scale[:, None, :]) + shift[:, None, :]


def swiglu(h, w_gate, w_up, w_down):
    return (jax.nn.silu(h @ w_gate) * (h @ w_up)) @ w_down


def depthwise_conv_centred(x, w, b):
    k = w.shape[0]
    y = lax.conv_general_dilated(x, w.astype(x.dtype)[:, None, :], window_strides=(1,),
                                 padding=[(k // 2, k // 2)],
                                 dimension_numbers=('NWC', 'WIO', 'NWC'),
                                 feature_group_count=x.shape[-1])
    return y + b


def grid_position_code(n_tokens, dtype):
    rows = n_tokens // GRID_W
    quarter = D_MODEL // 4
    freqs = jnp.exp(-math.log(POS_BASE) * jnp.arange(quarter, dtype=jnp.float32) / quarter)
    row = jnp.broadcast_to(jnp.arange(rows, dtype=jnp.float32)[:, None], (rows, GRID_W)).reshape(-1)
    col = jnp.broadcast_to(jnp.arange(GRID_W, dtype=jnp.float32)[None, :], (rows, GRID_W)).reshape(-1)
    ang_r = row[:, None] * freqs
    ang_c = col[:, None] * freqs
    return jnp.concatenate([jnp.sin(ang_r), jnp.cos(ang_r), jnp.sin(ang_c), jnp.cos(ang_c)],
                           axis=-1).astype(dtype)


def ssd_chunked(x, dt, a_neg, bmat, cmat, h0):
    b, L, H, P = x.shape
    G, N = bmat.shape[2], bmat.shape[3]
    E = H // G
    Q = SSD_CHUNK
    nc = L // Q
    f32 = jnp.float32
    dt = dt.astype(f32)
    xd = (x.astype(f32) * dt[..., None]).reshape(b, nc, Q, G, E, P)
    a_cs = jnp.cumsum((dt * a_neg).reshape(b, nc, Q, G, E), axis=2)
    bm = bmat.astype(f32).reshape(b, nc, Q, G, N)
    cm = cmat.astype(f32).reshape(b, nc, Q, G, N)
    seg = a_cs[:, :, :, None] - a_cs[:, :, None, :]
    lower = jnp.tril(jnp.ones((Q, Q), dtype=bool))[None, None, :, :, None, None]
    lmat = jnp.exp(jnp.where(lower, seg, -jnp.inf))
    scores = jnp.einsum('bclgn,bcsgn->bclsg', cm, bm)
    y_diag = jnp.einsum('bclsge,bcsgep->bclgep', scores[..., None] * lmat, xd)
    decay_to_end = jnp.exp(a_cs[:, :, -1:] - a_cs)
    states = jnp.einsum('bcsgn,bcsgep->bcgepn', bm, xd * decay_to_end[..., None])
    chunk_decay = jnp.exp(a_cs[:, :, -1])

    def step(h, inp):
        dec, st = inp
        return h * dec[..., None, None] + st, h

    h_last, h_prev = lax.scan(step, h0.astype(f32).reshape(b, G, E, P, N),
                              (jnp.moveaxis(chunk_decay, 1, 0), jnp.moveaxis(states, 1, 0)))
    h_prev = jnp.moveaxis(h_prev, 0, 1)
    y_off = jnp.einsum('bclgn,bcgepn->bclgep', cm, h_prev) * jnp.exp(a_cs)[..., None]
    y = (y_diag + y_off).reshape(b, L, H, P)
    return y, h_last.reshape(b, H, P, N)


def multiscale_pool(u, pool_w, pool_scale):
    b, L, _ = u.shape
    ug = u.astype(jnp.float32).reshape(b, L, N_POOL_GROUPS, POOL_GROUP_DIM)
    cs = jnp.concatenate([jnp.zeros((b, 1, N_POOL_GROUPS, POOL_GROUP_DIM), jnp.float32),
                          jnp.cumsum(ug, axis=1)], axis=1)
    t = jnp.arange(L)
    outs = []
    for gi, win in enumerate(POOL_WINDOWS):
        lo = jnp.clip(t - win // 2, 0, L)
        hi = jnp.clip(t - win // 2 + win, 0, L)
        cs_g = cs[:, :, gi]
        mean = (cs_g[:, hi] - cs_g[:, lo]) / (hi - lo).astype(jnp.float32)[None, :, None]
        outs.append(mean - ug[:, :, gi])
    pooled = jnp.stack(outs, axis=2)
    mixed = jnp.einsum('blgc,gcd->blgd', pooled, pool_w.astype(jnp.float32)).reshape(b, L, D_POOL)
    return (mixed * pool_scale.astype(jnp.float32)).astype(u.dtype)


def ssd_pool_mixer(h, h0, w_in, conv_w, conv_b, dt_bias, a_log, d_skip, ssd_norm_g,
                   pool_w, pool_scale, w_out):
    b, L, _ = h.shape
    z, xbc, dt_raw, pool_in = jnp.split(
        h @ w_in, [D_INNER_A, D_INNER_A + D_XBC, D_INNER_A + D_XBC + 2 * N_HEADS_A], axis=-1)
    xbc = jax.nn.silu(depthwise_conv_centred(xbc, conv_w, conv_b))
    xs, bm, cm = jnp.split(xbc, [D_INNER_A, D_INNER_A + N_GROUPS_A * D_STATE], axis=-1)
    xs = xs.reshape(b, L, N_HEADS_A, HEAD_DIM_A)
    bm = bm.reshape(b, L, N_GROUPS_A, D_STATE)
    cm = cm.reshape(b, L, N_GROUPS_A, D_STATE)
    dt = jax.nn.softplus(dt_raw.reshape(b, L, 2, N_HEADS_A).astype(jnp.float32)
                         + dt_bias.astype(jnp.float32))
    a_neg = -jnp.exp(a_log.astype(jnp.float32))
    y_f, hf = ssd_chunked(xs, dt[:, :, 0], a_neg[0], bm, cm, h0[:, 0])
    flip = lambda t: jnp.flip(t, axis=1)
    y_b, hb = ssd_chunked(flip(xs), flip(dt[:, :, 1]), a_neg[1], flip(bm), flip(cm), h0[:, 1])
    y = (y_f + flip(y_b)).astype(h.dtype) + d_skip[:, None] * xs
    y_a = rms_norm(y.reshape(b, L, D_INNER_A) * jax.nn.silu(z), ssd_norm_g)
    y_p = multiscale_pool(pool_in, pool_w, pool_scale)
    out = jnp.concatenate([y_a, y_p], axis=-1) @ w_out
    return out, jnp.stack([hf, hb], axis=1)


def gmlp_conv_mixer(h, w_in, v_ln_g, v_ln_b, sp_w, sp_b, dw_w, dw_b, cn_g, cn_b, w_out):
    b, L, _ = h.shape
    u, v, ga, gg = jnp.split(h @ w_in, [D_C, 2 * D_C, 2 * D_C + D_D], axis=-1)
    u = jax.nn.gelu(u)
    v = layer_norm(jax.nn.gelu(v), v_ln_g, v_ln_b)
    vc = v.reshape(b, L // MLP_CHUNK, MLP_CHUNK, N_HEADS_C, HEAD_DIM_C)
    sv = jnp.einsum('hts,bcshd->bcthd', sp_w, vc) + sp_b.T[None, None, :, :, None]
    y_c = u * sv.reshape(b, L, D_C)
    glu = ga * jax.nn.sigmoid(gg)
    y_d = jax.nn.silu(layer_norm(depthwise_conv_centred(glu, dw_w, dw_b), cn_g, cn_b))
    return jnp.concatenate([y_c, y_d], axis=-1) @ w_out


def setup_inputs(seed: int = 0) -> dict:
    key = jax.random.key(seed)
    k = jax.random.split(key, 32)
    f32 = jnp.float32

    def nrm(i, shape, scale=1.0):
        return jax.random.normal(k[i], shape, f32) * scale

    def near_one(i, shape):
        return 1.0 + 0.02 * jax.random.normal(k[i], shape, f32)

    dt0 = jnp.exp(jax.random.uniform(k[14], (N_EVEN, 2, N_HEADS_A), f32,
                                     math.log(1e-3), math.log(1e-1)))
    return {
        'x_prompt': nrm(0, (BATCH, SEQ, D_MODEL)),
        'x_sample': nrm(1, (DEC_BATCH, DEC_SEQ, D_MODEL)),
        'c': nrm(2, (DEC_BATCH, D_MODEL)),
        'state_ssd': nrm(3, (DEC_BATCH, N_EVEN, 2, N_HEADS_A, HEAD_DIM_A, D_STATE), 0.5),
        'c_ctx': nrm(4, (D_MODEL,)),
        'w_mod': nrm(5, (DEPTH, D_MODEL, N_MOD * D_MODEL), 0.5 * D_MODEL ** -0.5),
        'b_mod': nrm(6, (DEPTH, N_MOD * D_MODEL), 0.02),
        'norm_g': near_one(7, (DEPTH, 3, D_MODEL)),
        'ffn_w_gate': nrm(8, (DEPTH, 2, D_MODEL, D_FF), D_MODEL ** -0.5),
        'ffn_w_up': nrm(9, (DEPTH, 2, D_MODEL, D_FF), D_MODEL ** -0.5),
        'ffn_w_down': nrm(10, (DEPTH, 2, D_FF, D_MODEL), D_FF ** -0.5),
        'ev_w_in': nrm(11, (N_EVEN, D_MODEL, D_IN_EVEN), D_MODEL ** -0.5),
        'ev_conv_w': nrm(12, (N_EVEN, CONV_A, D_XBC), CONV_A ** -0.5),
        'ev_conv_b': nrm(13, (N_EVEN, D_XBC), 0.02),
        'ev_dt_bias': dt0 + jnp.log(-jnp.expm1(-dt0)),
        'ev_a_log': jnp.log(jax.random.uniform(k[15], (N_EVEN, 2, N_HEADS_A), f32, 1.0, 16.0)),
        'ev_d_skip': near_one(16, (N_EVEN, N_HEADS_A)),
        'ev_ssd_norm_g': near_one(17, (N_EVEN, D_INNER_A)),
        'ev_pool_w': nrm(18, (N_EVEN, N_POOL_GROUPS, POOL_GROUP_DIM, POOL_GROUP_DIM), POOL_GROUP_DIM ** -0.5),
        'ev_pool_scale': near_one(19, (N_EVEN, D_POOL)),
        'ev_w_out': nrm(20, (N_EVEN, D_OUT_EVEN, D_MODEL), D_OUT_EVEN ** -0.5),
        'od_w_in': nrm(21, (N_ODD, D_MODEL, D_IN_ODD), D_MODEL ** -0.5),
        'od_v_ln_g': near_one(22, (N_ODD, D_C)),
        'od_v_ln_b': nrm(23, (N_ODD, D_C), 0.02),
        'od_sp_w': nrm(24, (N_ODD, N_HEADS_C, MLP_CHUNK, MLP_CHUNK), MLP_CHUNK ** -0.5),
        'od_sp_b': near_one(25, (N_ODD, N_HEADS_C, MLP_CHUNK)),
        'od_dw_w': nrm(26, (N_ODD, CONV_D, D_D), CONV_D ** -0.5),
        'od_dw_b': nrm(27, (N_ODD, D_D), 0.02),
        'od_cn_g': near_one(28, (N_ODD, D_D)),
        'od_cn_b': nrm(29, (N_ODD, D_D), 0.02),
        'od_w_out': nrm(30, (N_ODD, D_OUT_ODD, D_MODEL), D_OUT_ODD ** -0.5),
        'final_norm_g': near_one(31, (D_MODEL,)),
    }


def reference(x_prompt, x_sample, c, state_ssd, c_ctx, w_mod, b_mod, norm_g,
              ffn_w_gate, ffn_w_up, ffn_w_down, ev_w_in, ev_conv_w, ev_conv_b,
              ev_dt_bias, ev_a_log, ev_d_skip, ev_ssd_norm_g, ev_pool_w, ev_pool_scale,
              ev_w_out, od_w_in, od_v_ln_g, od_v_ln_b, od_sp_w, od_sp_b, od_dw_w, od_dw_b,
              od_cn_g, od_cn_b, od_w_out, final_norm_g):

    def trunk(x, cond, ssd_init):
        sc = jax.nn.silu(cond)
        finals = []
        for i in range(DEPTH):
            m = (sc @ w_mod[i] + b_mod[i]).reshape(cond.shape[0], N_MOD, D_MODEL)
            hn = modulate(rms_norm(x, norm_g[i, 0]), m[:, 0], m[:, 1])
            x = x + 0.5 * m[:, 2][:, None] * swiglu(hn, ffn_w_gate[i, 0], ffn_w_up[i, 0], ffn_w_down[i, 0])
            hn = modulate(rms_norm(x, norm_g[i, 1]), m[:, 3], m[:, 4])
            j = i // 2
            if i % 2 == 0:
                mix, fin = ssd_pool_mixer(hn, ssd_init[:, j], ev_w_in[j], ev_conv_w[j], ev_conv_b[j],
                                          ev_dt_bias[j], ev_a_log[j], ev_d_skip[j], ev_ssd_norm_g[j],
                                          ev_pool_w[j], ev_pool_scale[j], ev_w_out[j])
                finals.append(fin)
            else:
                mix = gmlp_conv_mixer(hn, od_w_in[j], od_v_ln_g[j], od_v_ln_b[j], od_sp_w[j], od_sp_b[j],
                                      od_dw_w[j], od_dw_b[j], od_cn_g[j], od_cn_b[j], od_w_out[j])
            x = x + m[:, 5][:, None] * mix
            hn = modulate(rms_norm(x, norm_g[i, 2]), m[:, 6], m[:, 7])
            x = x + 0.5 * m[:, 8][:, None] * swiglu(hn, ffn_w_gate[i, 1], ffn_w_up[i, 1], ffn_w_down[i, 1])
        return rms_norm(x, final_norm_g), jnp.stack(finals, axis=1)

    b_ctx = x_prompt.shape[0]
    zero_state = jnp.zeros((b_ctx, N_EVEN, 2, N_HEADS_A, HEAD_DIM_A, D_STATE), x_prompt.dtype)
    y_prompt, ctx_states = trunk(x_prompt, c_ctx[None, :], zero_state)
    new_state_ssd = ctx_states.astype(x_prompt.dtype)

    x_lat = x_sample + grid_position_code(x_sample.shape[1], x_sample.dtype)[None]
    y_sample, _ = trunk(x_lat, c, state_ssd)

    return (y_prompt, y_sample, new_state_ssd)
```

```python
import math
import os
import numpy as np
import concourse.bass as bass
import concourse.mybir as mybir
from concourse.bass_utils import run_bass_kernel_spmd

F32 = mybir.dt.float32
BF16 = mybir.dt.bfloat16
AF = mybir.ActivationFunctionType
ALU = mybir.AluOpType

D = 1024; KC = 8; DFF = 2816; DEPTH = 4
NH = 16; HP = 64; NST = 128
DXBC = 1536; DIN_E = 3104; DIN_O = 4096
TP = 1024; TS = 2048
EPS = 1e-6
NEG = -30000.0


class T:
    def __init__(self, h, name):
        self.h = h; self.name = name; self.w = None; self.r = []

    def __getitem__(self, idx):
        return (self, self.h[idx])


def V(t, ap):
    return (t, ap)


class K:
    def __init__(self, nc):
        self.nc = nc
        self.eng = {'pe': nc.tensor, 'act': nc.scalar, 'dve': nc.vector, 'pool': nc.gpsimd, 'sp': nc.sync}
        self.sems = {}; self.cnt = {}
        self.known = {e: {} for e in self.eng}
        self._cms = []
        for e in self.eng:
            self._newsem(e)
        self.ninst = 0; self.nwait = 0; self.ndma = 0

    def _newsem(self, key):
        cm = self.nc.semaphore(str(key).replace(' ', ''))
        h = cm.__enter__(); self._cms.append(cm)
        self.sems[key] = h; self.cnt[key] = 0
        return h

    def _enter(self, cm):
        h = cm.__enter__(); self._cms.append(cm); return h

    def sbuf(self, name, shape, dt):
        return T(self._enter(self.nc.sbuf_tensor(name, list(shape), dt)), name)

    def psum(self, name, shape, dt=F32):
        return T(self._enter(self.nc.psum_tensor(name, list(shape), dt)), name)

    def dram(self, name, shape, dt):
        return T(self.nc.dram_tensor(name, list(shape), dt), name)

    def _wait(self, e, tok):
        key, val = tok
        if key[0] == 'd':
            val = max(val, self.cnt[key])
        if e == 'pe' and key == 'pe':
            return
        if self.known[e].get(key, 0) >= val:
            return
        self.eng[e].wait_ge(self.sems[key], val)
        self.known[e][key] = val
        self.nwait += 1

    def _deps(self, e, reads, writes):
        for t in reads:
            if t.w is not None:
                self._wait(e, t.w)
        for t in writes:
            if t.w is not None:
                self._wait(e, t.w)
            for tok in t.r:
                self._wait(e, tok)

    def _mark(self, tok, reads, writes):
        for t in reads:
            t.r.append(tok)
            if len(t.r) > 24:
                best = {}
                for kk, vv in t.r:
                    best[kk] = max(best.get(kk, 0), vv)
                t.r = list(best.items())
        for t in writes:
            t.w = tok; t.r = []

    @staticmethod
    def _split(args):
        ts, aps = [], []
        for a in args:
            if isinstance(a, tuple) and len(a) == 2 and isinstance(a[0], T):
                ts.append(a[0]); aps.append(a[1])
            else:
                aps.append(a)
        return ts, aps

    def op(self, e, fn, outs, ins, **kw):
        wt, oaps = self._split(outs)
        rt, iaps = self._split(ins)
        kw2 = {}
        for k_, v in kw.items():
            if isinstance(v, tuple) and len(v) == 2 and isinstance(v[0], T):
                if k_ == 'accum_out':
                    wt.append(v[0])
                else:
                    rt.append(v[0])
                kw2[k_] = v[1]
            else:
                kw2[k_] = v
        self._deps(e, rt, wt)
        inst = getattr(self.eng[e], fn)(*oaps, *iaps, **kw2)
        self.cnt[e] += 1
        inst.then_inc(self.sems[e], 1)
        self._mark((e, self.cnt[e]), rt, wt)
        self.ninst += 1
        return inst

    def dma(self, q, out, in_, **kw):
        wt, oaps = self._split([out])
        rt, iaps = self._split([in_])
        self._deps(q, rt, wt)
        t = (wt + rt)[0]
        semkey = ('d', t.name, q)
        if semkey not in self.sems:
            self._newsem(semkey)
        inst = self.eng[q].dma_start(out=oaps[0], in_=iaps[0], **kw)
        self.cnt[semkey] += 16
        inst.then_inc(self.sems[semkey], 16)
        self._mark((semkey, self.cnt[semkey]), rt, wt)
        self.ndma += 1
        return inst

    def collective(self, cin, cout):
        self._deps('pool', [cin], [cout])
        inst = self.nc.gpsimd.collective_compute(
            "AllGather", ALU.bypass, replica_groups=[[0, 1], [2, 3], [4, 5], [6, 7]],
            ins=[cin.h.ap()], outs=[cout.h.ap()])
        sk = ('c', 'cc')
        if sk not in self.sems:
            self._newsem(sk)
        self.cnt[sk] += 1
        inst.then_inc(self.sems[sk])
        self._mark((sk, self.cnt[sk]), [cin], [cout])

    def finish(self, e='sp'):
        for key in self.sems:
            if self.cnt[key] > 0 and self.known[e].get(key, 0) < self.cnt[key]:
                self.eng[e].wait_ge(self.sems[key], self.cnt[key])
                self.known[e][key] = self.cnt[key]

RA_NG = 0; RA_FN = 12; RA_BM = 13; RA_CW = 49; RA_CB = 69; RA_PS = 73; RA_CD = 75; RA_SG = 77; RA_VG = 79; RA_VB = 81; NRA = 83
RB_DW = 0; RB_DB = 62; RB_CG = 64; RB_CBB = 66; NRB = 68


def build(depth=DEPTH):
    KSTOP = int(os.environ.get('KSTOP', '9')); KPASS = os.environ.get('KPASS', 'PS'); KSUB = int(os.environ.get('KSUB', '9'))
    nc = bass.Bass("TRN2", target_bir_lowering=False)
    k = K(nc)

    def din(name, shape):
        return nc.dram_tensor(name, list(shape), F32, kind="ExternalInput").ap()

    def dout(name, shape):
        return nc.dram_tensor(name, list(shape), F32, kind="ExternalOutput").ap()

    xp_d = din("xp", [TP, D]); xs_d = din("xs", [TS, D]); pos_d = din("pos", [TS, D])
    st_d = din("st", [2, 2, NH * HP, NST])
    bankA_d = din("bankA", [NRA, D]); bankB_d = din("bankB", [NRB, D])
    cst_d = din("cst", [128, 5, 128]); eye_d = din("eye16", [16, 16]); mab_d = din("mab", [128, 2])
    icp_d = din("icp", [128, 4, 256]); ics_d = din("ics", [128, 4, TS])
    spb_d = din("spb", [2, 2, D]); rows_d = din("rows", [2, 3, 32])
    z_dummy = None
    w_mod = din("w_mod", [DEPTH, D, 9 * D])
    wg_d = din("ffn_w_gate", [DEPTH, 2, D, DFF]); wu_d = din("ffn_w_up", [DEPTH, 2, D, DFF])
    wd_d = din("ffn_w_down", [DEPTH, 2, DFF, D])
    ev_w_in = din("ev_w_in", [2, D, DIN_E]); ev_pool_w = din("ev_pool_w", [2, 4, 128, 128])
    ev_w_out = din("ev_w_out", [2, 1536, D])
    od_w_in = din("od_w_in", [2, D, DIN_O]); od_sp_w = din("od_sp_w", [2, 8, 128, 128])
    od_w_out = din("od_w_out", [2, 2048, D])
    yp_d = dout("yp", [TP, D]); ys_d = dout("ys", [TS, D]); ns_d = dout("ns", [4, 2, 2, NH * HP, NST])

    x = k.sbuf("x", [128, 8, TS], F32)
    hn = k.sbuf("hn", [128, 8, TS], BF16)
    WR = [k.sbuf(f"wr{i}", [128, 4096], BF16) for i in range(6)]
    FR = [k.sbuf(f"fr{i}", [128, 512], F32) for i in range(5)]
    rst = k.sbuf("rst", [128, 512], F32)
    BR = [k.sbuf(f"br{i}", [128, 1024], BF16) for i in range(5)]
    XR = [k.sbuf(f"xr{i}", [128, 1024], F32) for i in range(2)]
    PS = [k.psum(f"ps{i}", [128, 512], F32) for i in range(4)]
    YP = [k.psum(f"yp{i}", [128, 512], F32) for i in range(2)]
    PB = [k.psum(f"pb{i}", [128, 1024], BF16) for i in range(2)]
    ctr = {}

    def ring(name, lst):
        i = ctr.get(name, 0); ctr[name] = i + 1
        return lst[i % len(lst)]

    wr = lambda: ring('wr', WR)
    fr = lambda: ring('fr', FR)
    br = lambda: ring('br', BR)
    xr = lambda: ring('xr', XR)
    ps = lambda: ring('ps', PS)
    pb = lambda: ring('pb', PB)

    cst = k.sbuf("cst_sb", [128, 5, 128], F32)
    identb = k.sbuf("identb", [128, 128], BF16)
    onesb = k.sbuf("onesb", [128, 128], BF16)
    onesf = k.sbuf("onesf", [128, 128], F32)
    eye16 = k.sbuf("eye16_sb", [16, 16], F32)
    mab = k.sbuf("mab_sb", [128, 2], F32)
    vta = k.sbuf("vta", [128, 8, NRA], F32)
    vtb = k.sbuf("vtb", [128, 8, NRB], F32)
    modv = k.sbuf("modv", [128, DEPTH, 72, 2], F32)
    sct = k.sbuf("sct", [128, 8, 2], BF16)
    rows = k.sbuf("rows_sb", [128, 3, 32], F32)
    sm = k.sbuf("sm", [128, 8, 8], F32)

    k.dma('sp', cst[:], cst_d)
    k.dma('sp', eye16[:], eye_d)
    k.dma('sp', mab[:], mab_d)
    ident = cst[:, 0, :]
    k.op('dve', 'tensor_copy', [identb[:]], [cst[:, 0, :]])
    k.op('dve', 'memset', [onesb[:]], [1.0])
    k.op('dve', 'memset', [onesf[:]], [1.0])

    def load_bank(bank_d, n, vt):
        bk = xr()
        k.dma('sp', bk[0:n, :], bank_d)
        for c in range(8):
            p = ps()
            k.op('pe', 'transpose', [p[:, 0:n]], [bk[0:n, c * 128:(c + 1) * 128], cst[0:n, 0, 0:n]])
            k.op('dve', 'tensor_copy', [vt[:, c, :]], [p[:, 0:n]])

    load_bank(bankA_d, NRA, vta)
    load_bank(bankB_d, NRB, vtb)

    def vA(r, c):
        return vta[:, c, r:r + 1]

    def vB(r, c):
        return vtb[:, c, r:r + 1]

    k.op('act', 'activation', [sct[:]], [vta[:, :, RA_CD:RA_CD + 2], AF.Silu])
    for i in range(depth):
        wv = w_mod[i].rearrange("(kc p) n -> p kc n", p=128)
        for blk in range(18):
            s = wr()
            sv = V(s, s.h[:, :].rearrange("p (kc n) -> p kc n", kc=8))
            k.dma('pool', sv, wv[:, :, blk * 512:(blk + 1) * 512])
            p = ps()
            for q in range(4):
                for kc in range(8):
                    k.op('pe', 'matmul', [p[:, q * 2:q * 2 + 2]],
                         [V(s, sv[1][:, kc, q * 128:(q + 1) * 128]), sct[:, kc, :]], start=(kc == 0), stop=(kc == 7))
            fc0 = blk * 4
            kk = fc0 // 8; c0 = fc0 % 8
            bias = V(vta, vta.h[:, c0:c0 + 4, RA_BM + i * 9 + kk].unsqueeze(2).to_broadcast([128, 4, 2]))
            k.op('dve', 'tensor_tensor', [modv[:, i, fc0:fc0 + 4, :]],
                 [V(p, p.h[:, 0:8].rearrange("p (a b) -> p a b", b=2)), bias, ALU.add])

    def mvec(i, kk, cond):
        return V(modv, modv.h[:, i, kk * 8:(kk + 1) * 8, cond])

    def rstd_tile(src_fn, tw, scale=1.0 / D):
        sq = wr()
        sqv = V(sq, sq.h[:, :].rearrange("p (c n) -> p c n", c=8))
        for c in range(8):
            k.op('act', 'activation', [V(sq, sqv[1][:, c, 0:tw])], [src_fn(c), AF.Square])
        p = ps()
        for c in range(8):
            k.op('pe', 'matmul', [p[:, 0:tw]], [onesb[:], V(sq, sqv[1][:, c, 0:tw])], start=(c == 0), stop=(c == 7))
        r = rst
        k.op('act', 'activation', [r[:, 0:tw]], [p[:, 0:tw], AF.Ln], bias=EPS, scale=scale)
        k.op('act', 'activation', [r[:, 0:tw]], [r[:, 0:tw], AF.Exp], scale=-0.5)
        return r

    def normmod(i, which, cond, TN):
        g_ = V(vta, vta.h[:, :, RA_NG + i * 3 + which])
        k.op('dve', 'scalar_tensor_tensor', [sm[:, :, 0]], [mvec(i, 3 * which + 1, cond), 1.0, g_], op0=ALU.add, op1=ALU.mult)
        k.op('dve', 'tensor_copy', [sm[:, :, 1]], [mvec(i, 3 * which, cond)])
        for t0 in range(0, TN, 512):
            r = rstd_tile(lambda c: x[:, c, t0:t0 + 512], 512)
            for c in range(8):
                tmp = fr()
                k.op('dve', 'tensor_tensor', [tmp[:]], [x[:, c, t0:t0 + 512], r[:], ALU.mult])
                k.op('act', 'activation', [hn[:, c, t0:t0 + 512]], [tmp[:], AF.Identity],
                     scale=sm[:, c, 0:1], bias=sm[:, c, 1:2])

    def ffn(i, f, cond, TN):
        kk = 2 if f == 0 else 8
        k.op('dve', 'tensor_scalar', [sm[:, :, 2]], [mvec(i, kk, cond), 0.5, None, ALU.mult])
        blocks = [(c0, min(512, DFF - c0)) for c0 in range(0, DFF, 512)]
        wgv = wg_d[i, f].rearrange("(kc p) n -> p kc n", p=128)
        wuv = wu_d[i, f].rearrange("(kc p) n -> p kc n", p=128)
        wdv = wd_d[i, f].rearrange("(j p) n -> p j n", p=128)

        def load(bi):
            c0, cw = blocks[bi]
            sg, su, sd = wr(), wr(), wr()
            sgv = sg.h[:, 0:8 * cw].rearrange("p (kc n) -> p kc n", kc=8)
            suv = su.h[:, 0:8 * cw].rearrange("p (kc n) -> p kc n", kc=8)
            nj = cw // 128
            sdv = sd.h[:, 0:nj * 1024].rearrange("p (j n) -> p j n", j=nj)
            k.dma('pool', V(sg, sgv), wgv[:, :, c0:c0 + cw])
            k.dma('pool', V(su, suv), wuv[:, :, c0:c0 + cw])
            k.dma('pool', V(sd, sdv), wdv[:, c0 // 128:c0 // 128 + nj, :])
            return (sg, sgv, su, suv, sd, sdv, nj)

        nxt = load(0)
        for bi in range(len(blocks)):
            cur = nxt
            if bi + 1 < len(blocks):
                nxt = load(bi + 1)
            sg, sgv, su, suv, sd, sdv, nj = cur
            for t0 in range(0, TN, 512):
                hc = [br() for _ in range(2)]
                hcv = lambda j: hc[j // 2][:, (j % 2) * 512:(j % 2 + 1) * 512]
                for j in range(nj):
                    pg = ps(); pu = ps()
                    for kc in range(8):
                        k.op('pe', 'matmul', [pg[:]], [V(sg, sgv[:, kc, j * 128:(j + 1) * 128]), hn[:, kc, t0:t0 + 512]],
                             start=(kc == 0), stop=(kc == 7))
                    for kc in range(8):
                        k.op('pe', 'matmul', [pu[:]], [V(su, suv[:, kc, j * 128:(j + 1) * 128]), hn[:, kc, t0:t0 + 512]],
                             start=(kc == 0), stop=(kc == 7))
                    sgt = fr()
                    k.op('act', 'activation', [sgt[:]], [pg[:], AF.Silu])
                    k.op('dve', 'tensor_tensor', [hcv(j)], [sgt[:], pu[:], ALU.mult])
                for oc in range(8):
                    pd = ps()
                    for j in range(nj):
                        k.op('pe', 'matmul', [pd[:]], [V(sd, sdv[:, j, oc * 128:(oc + 1) * 128]), hcv(j)],
                             start=(j == 0), stop=(j == nj - 1))
                    k.op('dve', 'scalar_tensor_tensor', [x[:, oc, t0:t0 + 512]],
                         [pd[:], sm[:, oc, 2:3], x[:, oc, t0:t0 + 512]], op0=ALU.mult, op1=ALU.add)

    def load_x(src_d, TN, addpos):
        for g in range(TN // 128):
            t = xr()
            k.dma('sp', t[:], src_d[g * 128:(g + 1) * 128, :])
            if addpos:
                t2 = xr()
                k.dma('sp', t2[:], pos_d[g * 128:(g + 1) * 128, :])
                k.op('dve', 'tensor_tensor', [t[:]], [t[:], t2[:], ALU.add])
            for h in range(2):
                p = ps()
                for q in range(4):
                    c = h * 4 + q
                    k.op('pe', 'transpose', [p[:, q * 128:(q + 1) * 128]], [t[:, c * 128:(c + 1) * 128], cst[:, 0, :]])
                k.op('act', 'copy', [x[:, h * 4:(h + 1) * 4, g * 128:(g + 1) * 128]],
                     [V(p, p.h[:, :].rearrange("p (q n) -> p q n", q=4))])

    def final_store(dst_d, TN):
        for t0 in range(0, TN, 512):
            r = rstd_tile(lambda c: x[:, c, t0:t0 + 512], 512)
            for g in range(4):
                t = xr()
                for c in range(8):
                    tmp = fr()
                    k.op('dve', 'scalar_tensor_tensor', [tmp[:, 0:128]],
                         [x[:, c, t0 + g * 128:t0 + (g + 1) * 128], vA(RA_FN, c), r[:, g * 128:(g + 1) * 128]],
                         op0=ALU.mult, op1=ALU.mult)
                    p = ps()
                    k.op('pe', 'transpose', [p[:, 0:128]], [tmp[:, 0:128], cst[:, 0, :]])
                    k.op('act', 'copy', [t[:, c * 128:(c + 1) * 128]], [p[:, 0:128]])
                k.dma('sp', dst_d[t0 + g * 128:t0 + (g + 1) * 128, :], t[:])


    WP = TS + 32
    raw_d = k.dram("raw_d", [128, 16, WP], BF16)
    xbc_d = k.dram("xbc_d", [128, 12, TS], BF16)
    hpb_d = k.dram("hpb_d", [16, 128, 1024], BF16)
    z_d = k.dram("z_d", [TS, 1024], BF16)
    glu_d = k.dram("glu_d", [128, 8, WP + 128], BF16)
    dts = k.sbuf("dts", [128, 16, 32], F32)
    acs = k.sbuf("acs", [128, 16, 32], F32)
    pbs = k.sbuf("pbs", [128, 16, 16], F32)
    sA = k.sbuf("sA", [128, 32], F32); sE = k.sbuf("sE", [128, 32], F32); sD = k.sbuf("sD", [128, 32], F32)
    sF = k.sbuf("sF", [128, 32], F32); sC = k.sbuf("sC", [128, 32], F32)
    spw = k.sbuf("spw", [128, 8, 128], BF16)
    spbb = k.sbuf("spbb", [128, 1024], F32)
    zb = k.sbuf("zb", [128, 16, 16], BF16)
    facc = k.sbuf("facc", [128, 1024], F32)
    ya = [k.sbuf("ya0", [128, 512], F32), k.sbuf("ya1", [128, 512], F32)]
    sc_t = k.sbuf("sc_t", [128, 256], BF16)
    pw = k.sbuf("pw", [128, 512], BF16)
    ssq = k.sbuf("ssq", [128, 4], F32)
    pf = k.sbuf("pf", [128, 16], F32); pbr = k.sbuf("pbr", [128, 16], F32)
    k.op('dve', 'memset', [zb[:]], [0.0])
    zsrc = WR[5]
    k.op('dve', 'memset', [zsrc[:]], [0.0])
    for c_ in range(16):
        k.dma('sp', raw_d[:, c_, :], zsrc[:, 0:WP])
    for c_ in range(8):
        k.dma('sp', glu_d[:, c_, :], zsrc[:, 0:WP + 128])
    for c_ in range(12):
        k.dma('sp', xbc_d[:, c_, :], zsrc[:, 0:TS])
    for c_ in range(16):
        k.dma('sp', hpb_d[c_, :, :], zsrc[:, 0:1024])
        k.dma('sp', z_d[c_ * 128:(c_ + 1) * 128, :], zsrc[:, 0:1024])
    exn = [0]

    def exchange(src, F):
        n = exn[0]; exn[0] += 1
        cin = k.dram(f"cin{n}", [128, F], F32); cout = k.dram(f"cout{n}", [256, F], F32)
        k.dma('sp', cin[:, :], src)
        k.collective(cin, cout)
        g = fr()
        k.dma('sp', V(g, g.h[:, 0:2 * F].rearrange("p (r f) -> p r f", r=2)),
              V(cout, cout.h.ap().rearrange("(r p) f -> p r f", p=128)))
        o = fr()
        k.op('dve', 'tensor_scalar', [o[:, 0:F]], [g[:, 0:F], mab[:, 1:2], None, ALU.mult])
        k.op('dve', 'scalar_tensor_tensor', [o[:, 0:F]], [g[:, F:2 * F], mab[:, 0:1], o[:, 0:F]], op0=ALU.mult, op1=ALU.add)
        return o

    def halo_exchange(dram, nchunk, padw, L):
        F = nchunk * 2 * padw
        eb = br()
        ebv = eb.h[:, 0:F].rearrange("p (c w) -> p c w", w=2 * padw)
        k.dma('sp', V(eb, ebv[:, :, 0:padw]), dram[:, 0:nchunk, padw:2 * padw])
        k.dma('sp', V(eb, ebv[:, :, padw:2 * padw]), dram[:, 0:nchunk, L:L + padw])
        ef = fr()
        k.op('dve', 'tensor_copy', [ef[:, 0:F]], [eb[:, 0:F]])
        part = exchange(ef[:, 0:F], F)
        pv = part.h[:, 0:F].rearrange("p (c w) -> p c w", w=2 * padw)
        pads = br()
        pdv = pads.h[:, 0:F].rearrange("p (c w) -> p c w", w=2 * padw)
        k.op('dve', 'tensor_scalar', [V(pads, pdv[:, :, 0:padw])], [V(part, pv[:, :, padw:2 * padw]), mab[:, 1:2], None, ALU.mult])
        k.op('dve', 'tensor_scalar', [V(pads, pdv[:, :, padw:2 * padw])], [V(part, pv[:, :, 0:padw]), mab[:, 0:1], None, ALU.mult])
        k.dma('sp', dram[:, 0:nchunk, 0:padw], V(pads, pdv[:, :, 0:padw]))
        k.dma('sp', dram[:, 0:nchunk, padw + L:2 * padw + L], V(pads, pdv[:, :, padw:2 * padw]))

    def zero_pads(dram, nchunk, padw, L, nseg):
        SW = L + 2 * padw
        for s_i in range(nseg):
            k.dma('sp', dram[:, 0:nchunk, s_i * SW:s_i * SW + padw], zb[:, 0:nchunk, 0:padw])
            k.dma('sp', dram[:, 0:nchunk, s_i * SW + padw + L:(s_i + 1) * SW], zb[:, 0:nchunk, 0:padw])

    def store_padded(dram, ch, st, t0, L, padw):
        SW = L + 2 * padw
        if L >= 512:
            k.dma('sp', dram[:, ch, padw + t0:padw + t0 + 512], st)
        else:
            s0 = t0 // L; ns_ = 512 // L
            dst = dram.h.ap()[:, ch, s0 * SW:(s0 + ns_) * SW].rearrange("p (s w) -> p s w", w=SW)[:, :, padw:padw + L]
            k.dma('sp', V(dram, dst), V(st[0], st[1].rearrange("p (s w) -> p s w", w=L)))

    def proj_fm(wv_cols, nchk, TN, consume):
        s = wr()
        sv = s.h[:, :].rearrange("p (kc n) -> p kc n", kc=8)
        k.dma('pool', V(s, sv[:, :, 0:nchk * 128]), wv_cols)
        for t0 in range(0, TN, 512):
            for q in range(nchk):
                p = ps()
                for kc in range(8):
                    k.op('pe', 'matmul', [p[:]], [V(s, sv[:, kc, q * 128:(q + 1) * 128]), hn[:, kc, t0:t0 + 512]],
                         start=(kc == 0), stop=(kc == 7))
                consume(p, q, t0)

    def bc16(t, ap16):
        return V(t, ap16.unsqueeze(2).to_broadcast([128, 16, HP]))

    def bc8(t, ap8):
        return V(t, ap8.unsqueeze(2).to_broadcast([128, 8, HP]))

    def v3(tv):
        return V(tv[0], tv[1].rearrange("p (h q) -> p h q", q=HP))

    def even_mixer(i, j, cond, L, nseg, sample, seq0):
        TN = L * nseg; nch = L // 128; SW = L + 16
        win = ev_w_in[j].rearrange("(kc p) n -> p kc n", p=128)
        k.dma('sp', V(rows, rows.h[:, :, :].rearrange("p a b -> p (a b)").unsqueeze(1)),
              rows_d.rearrange("j a b -> j (a b)")[j:j + 1, :].partition_broadcast(128))
        k.op('act', 'activation', [rows[:, 1, :]], [rows[:, 1, :], AF.Exp])
        k.op('dve', 'tensor_scalar', [rows[:, 1, :]], [rows[:, 1, :], -1.0, None, ALU.mult])
        k.op('dve', 'tensor_copy', [sm[:, :, 3]], [mvec(i, 5, cond)])

        def raw_consume(cb):
            def f(p, q, t0):
                st = br()
                k.op('act', 'copy', [st[:, 0:512]], [p[:]])
                store_padded(raw_d, cb + q, st[:, 0:512], t0, L, 8)
            return f
        for b_ in range(3):
            proj_fm(win[:, :, D + b_ * 512:D + (b_ + 1) * 512], 4, TN, raw_consume(b_ * 4))
        proj_fm(win[:, :, D + DXBC + 32:D + DXBC + 32 + 512], 4, TN, raw_consume(12))
        if sample:
            halo_exchange(raw_d, 16, 8, L)
        else:
            zero_pads(raw_d, 16, 8, L, nseg)
        for g in range(2):
            s = wr()
            sv = s.h[:, :].rearrange("p (kc n) -> p kc n", kc=8)
            k.dma('pool', V(s, sv), win[:, :, g * 512:(g + 1) * 512])
            for gc in range(TN // 128):
                p = ps()
                for kc in range(8):
                    k.op('pe', 'matmul', [p[:]], [hn[:, kc, gc * 128:(gc + 1) * 128], V(s, sv[:, kc, :])], start=(kc == 0), stop=(kc == 7))
                st = br()
                k.op('act', 'activation', [st[:, 0:512]], [p[:], AF.Silu])
                k.dma('sp', z_d[gc * 128:(gc + 1) * 128, g * 512:(g + 1) * 512], st[:, 0:512])
        s = wr()
        sv = s.h[:, 0:256].rearrange("p (kc n) -> p kc n", kc=8)
        k.dma('pool', V(s, sv), win[:, :, D + DXBC:D + DXBC + 32])
        for gc in range(TN // 128):
            p = ps()
            for kc in range(8):
                k.op('pe', 'matmul', [p[:, 0:32]], [hn[:, kc, gc * 128:(gc + 1) * 128], V(s, sv[:, kc, :])],
                     start=(kc == 0), stop=(kc == 7))
            tmp = fr()
            k.op('dve', 'tensor_tensor', [tmp[:, 0:32]], [p[:, 0:32], rows[:, 0, :], ALU.add])
            k.op('act', 'activation', [tmp[:, 0:32]], [tmp[:, 0:32], AF.Exp])
            k.op('act', 'activation', [dts[:, gc, :]], [tmp[:, 0:32], AF.Ln], bias=1.0, scale=1.0)
        if KSTOP <= 1:
            return
        for s_i in range(nseg):
            for t0 in range(0, L, 512):
                tw = min(512, L - t0)
                for half in range(2):
                    rt = wr()
                    rv = rt.h[:, 0:6 * (tw + 4)].rearrange("p (c w) -> p c w", c=6)
                    col0 = s_i * SW + 8 + t0 - 2
                    k.dma('sp', V(rt, rv), raw_d[:, half * 6:(half + 1) * 6, col0:col0 + tw + 4])
                    for c6 in range(6):
                        c = half * 6 + c6
                        hi = 1 if c >= 8 else 0
                        acc = fr()
                        k.op('dve', 'tensor_scalar', [acc[:, 0:tw]],
                             [V(rt, rv[:, c6, 0:tw]), vA(RA_CW + (j * 5) * 2 + hi, c % 8), None, ALU.mult])
                        for kk in range(1, 5):
                            k.op('dve', 'scalar_tensor_tensor', [acc[:, 0:tw]],
                                 [V(rt, rv[:, c6, kk:kk + tw]), vA(RA_CW + (j * 5 + kk) * 2 + hi, c % 8), acc[:, 0:tw]],
                                 op0=ALU.mult, op1=ALU.add)
                        ob = br()
                        k.op('act', 'activation', [ob[:, 0:tw]], [acc[:, 0:tw], AF.Silu],
                             bias=vA(RA_CB + j * 2 + hi, c % 8), scale=1.0)
                        k.dma('sp', xbc_d[:, c, s_i * L + t0:s_i * L + t0 + tw], ob[:, 0:tw])

        if KSTOP <= 2:
            return
        A0, A1, A2 = WR[0], WR[1], WR[2]
        xs_t = V(A0, A0.h[:, 0:1024]); xdf = V(A0, A0.h[:, 1024:2048]); xdb = V(A0, A0.h[:, 2048:3072]); xdd = V(A0, A0.h[:, 3072:4096])
        xbv = A1.h[:, 0:1536].rearrange("p (c w) -> p c w", c=12)
        bt = V(A1, A1.h[:, 1536:1792]); hpf = V(A1, A1.h[:, 2048:3072]); hpb = V(A1, A1.h[:, 3072:4096])
        ycv = A2.h[:, 0:1536].rearrange("p (c w) -> p c w", c=12)
        pinv = A2.h[:, 2048:2048 + 576].rearrange("p (g w) -> p g w", g=4)
        zsv = V(A2, A2.h[:, 3072:4096])
        hf, hb = XR[0], XR[1]
        WO = [WR[3], WR[4], WR[5]]
        wov = ev_w_out[j].rearrange("(kc p) n -> p kc n", p=128)
        for q in range(3):
            k.dma('pool', V(WO[q], WO[q].h[:, :].rearrange("p (kc n) -> p kc n", kc=4)), wov[:, q * 4:(q + 1) * 4, :])
        k.dma('pool', V(pw, pw.h[:, :].rearrange("p (g n) -> p g n", g=4)), ev_pool_w[j].rearrange("g c d -> c g d"))

        def prep(gc):
            k.dma('sp', V(A1, xbv), xbc_d[:, :, gc * 128:(gc + 1) * 128])
            p = pb()
            for c in range(8):
                k.op('pe', 'transpose', [p[:, c * 128:(c + 1) * 128]], [V(A1, xbv[:, c, :]), identb[:]])
            k.op('act', 'copy', [xs_t], [p[:]])
            p2 = pb()
            for g in range(2):
                k.op('pe', 'transpose', [p2[:, g * 128:(g + 1) * 128]], [V(A1, xbv[:, 8 + g, :]), identb[:]])
            k.op('act', 'copy', [bt], [p2[:, 0:256]])
            k.op('dve', 'tensor_tensor', [sA[:]], [dts[:, gc, :], rows[:, 1, :], ALU.mult])
            p3 = ps()
            k.op('pe', 'matmul', [p3[:, 0:16]], [cst[:, 1, :], sA[:, 0:16]], start=True, stop=True)
            k.op('pe', 'matmul', [p3[:, 16:32]], [cst[:, 2, :], sA[:, 16:32]], start=True, stop=True)
            k.op('pe', 'matmul', [p3[:, 32:64]], [onesf[:], sA[:]], start=True, stop=True)
            k.op('dve', 'tensor_copy', [acs[:, gc, :]], [p3[:, 0:32]])
            k.op('act', 'activation', [sE[:]], [p3[:, 0:32], AF.Exp])
            k.op('dve', 'tensor_tensor', [sD[:]], [p3[:, 32:64], acs[:, gc, :], ALU.subtract])
            k.op('act', 'activation', [sD[:]], [sD[:], AF.Exp])
            k.op('dve', 'tensor_tensor', [sF[:]], [sD[:], dts[:, gc, :], ALU.mult])
            k.op('act', 'activation', [sC[:]], [p3[:, 32:64], AF.Exp])

        def chunk_state(d):
            k.op('dve', 'tensor_tensor', [v3(xdd)], [v3(xs_t), bc16(sF, sF.h[:, d * 16:(d + 1) * 16]), ALU.mult])
            out = []
            for g in range(2):
                p = ps()
                k.op('pe', 'matmul', [p[:]], [V(bt[0], bt[1][:, g * 128:(g + 1) * 128]), V(xdd[0], xdd[1][:, g * 512:(g + 1) * 512])],
                     start=True, stop=True)
                out.append(p)
            return out

        def state_upd(h_t, d):
            pp = chunk_state(d)
            k.op('dve', 'tensor_tensor', [v3(h_t[:, :])], [v3(h_t[:, :]), bc16(sC, sC.h[:, d * 16:(d + 1) * 16]), ALU.mult])
            for g in range(2):
                k.op('dve', 'tensor_tensor', [h_t[:, g * 512:(g + 1) * 512]], [h_t[:, g * 512:(g + 1) * 512], pp[g][:], ALU.add])

        def load_state(h_t, d, mcol):
            if mcol is None:
                k.op('dve', 'memset', [h_t[:]], [0.0])
                return
            for c in range(8):
                t = fr()
                k.dma('sp', t[:, 0:128], st_d[j, d, c * 128:(c + 1) * 128, :])
                p = ps()
                k.op('pe', 'transpose', [p[:, 0:128]], [t[:, 0:128], cst[:, 0, :]])
                k.op('dve', 'tensor_scalar', [h_t[:, c * 128:(c + 1) * 128]], [p[:, 0:128], mab[:, mcol:mcol + 1], None, ALU.mult])

        def store_state(h_t, seq, d):
            for c in range(8):
                p = ps()
                k.op('pe', 'transpose', [p[:, 0:128]], [h_t[:, c * 128:(c + 1) * 128], cst[:, 0, :]])
                t = fr()
                k.op('act', 'copy', [t[:, 0:128]], [p[:, 0:128]])
                k.dma('sp', ns_d[seq, j, d, c * 128:(c + 1) * 128, :], t[:, 0:128])

        for s_i in range(nseg):
            gc0 = s_i * nch
            load_state(hb, 1, 1 if sample else None)
            k.op('dve', 'memset', [pbr[:]], [1.0])
            if sample:
                k.op('dve', 'memset', [facc[:]], [0.0])
                k.op('dve', 'memset', [pf[:]], [1.0])
            for c in reversed(range(nch)):
                gc = gc0 + c
                prep(gc)
                k.op('act', 'copy', [hpb], [hb[:]])
                k.dma('sp', hpb_d[c, :, :], hpb)
                k.op('dve', 'tensor_copy', [pbs[:, c, :]], [pbr[:]])
                k.op('dve', 'tensor_tensor', [pbr[:]], [pbr[:], sC[:, 16:32], ALU.mult])
                state_upd(hb, 1)
                if sample:
                    k.op('dve', 'tensor_tensor', [v3(xdd)], [v3(xs_t), bc16(sF, sF.h[:, 0:16]), ALU.mult])
                    k.op('dve', 'tensor_tensor', [v3(xdd)], [v3(xdd), bc16(pf, pf.h[:, :]), ALU.mult])
                    for g in range(2):
                        p = ps()
                        k.op('pe', 'matmul', [p[:]], [V(bt[0], bt[1][:, g * 128:(g + 1) * 128]), V(xdd[0], xdd[1][:, g * 512:(g + 1) * 512])],
                             start=True, stop=True)
                        k.op('dve', 'tensor_tensor', [facc[:, g * 512:(g + 1) * 512]], [facc[:, g * 512:(g + 1) * 512], p[:], ALU.add])
                    k.op('dve', 'tensor_tensor', [pf[:]], [pf[:], sC[:, 0:16], ALU.mult])
            if not sample:
                store_state(hb, seq0 + s_i, 1)
                load_state(hf, 0, None)
            else:
                load_state(hf, 0, 0)
                k.op('dve', 'tensor_tensor', [v3(hf[:, :])], [v3(hf[:, :]), bc16(pf, pf.h[:, :]), ALU.mult])
                k.op('dve', 'scalar_tensor_tensor', [facc[:]], [facc[:], mab[:, 0:1], hf[:]], op0=ALU.mult, op1=ALU.add)
                k.op('dve', 'scalar_tensor_tensor', [facc[:]], [hb[:], mab[:, 1:2], facc[:]], op0=ALU.mult, op1=ALU.add)
                for q in range(4):
                    o = exchange(facc[:, q * 256:(q + 1) * 256], 256)
                    k.op('dve', 'tensor_copy', [facc[:, q * 256:(q + 1) * 256]], [o[:, 0:256]])
                load_state(hf, 0, 0)
                k.op('dve', 'scalar_tensor_tensor', [hf[:]], [facc[:], mab[:, 1:2], hf[:]], op0=ALU.mult, op1=ALU.add)
                k.op('dve', 'tensor_scalar', [facc[:]], [facc[:], mab[:, 0:1], None, ALU.mult])
            if KSTOP <= 3:
                continue
            for c in range(nch):
                gc = gc0 + c
                tg = gc * 128
                prep(gc)
                k.op('act', 'copy', [hpf], [hf[:]])
                k.dma('sp', hpb, hpb_d[c, :, :])
                k.dma('sp', zsv, z_d[tg:tg + 128, :])
                if sample:
                    for g in range(2):
                        t = fr()
                        k.op('dve', 'tensor_tensor', [v3(t[:, :])], [v3(facc[:, g * 512:(g + 1) * 512]), bc8(pbs, pbs.h[:, c, g * 8:(g + 1) * 8]), ALU.mult])
                        k.op('dve', 'tensor_tensor', [V(hpb[0], hpb[1][:, g * 512:(g + 1) * 512])], [V(hpb[0], hpb[1][:, g * 512:(g + 1) * 512]), t[:], ALU.add])
                k.op('dve', 'tensor_tensor', [v3(xdf)], [v3(xs_t), bc16(dts, dts.h[:, gc, 0:16]), ALU.mult])
                k.op('dve', 'tensor_tensor', [v3(xdb)], [v3(xs_t), bc16(dts, dts.h[:, gc, 16:32]), ALU.mult])
                for g in range(2):
                    p = ps()
                    k.op('pe', 'matmul', [p[:, 0:128]], [V(A1, xbv[:, 8 + g, :]), V(A1, xbv[:, 10 + g, :])], start=True, stop=True)
                    k.op('act', 'copy', [sc_t[:, g * 128:(g + 1) * 128]], [p[:, 0:128]])
                if KSTOP <= 4:
                    continue
                k.op('dve', 'tensor_tensor', [v3(xdd)], [v3(xs_t), bc16(rows, rows.h[:, 2, 0:16]), ALU.mult])
                for h_ in range(NH):
                    g = h_ // 8
                    yslice = YP[g][:, (h_ % 8) * 64:(h_ % 8 + 1) * 64]
                    for d in range(2):
                        dh = fr()
                        k.op('dve', 'tensor_scalar', [dh[:, 0:128]], [onesf[:, :], sA[:, d * 16 + h_:d * 16 + h_ + 1], None, ALU.mult])
                        p = ps()
                        k.op('pe', 'matmul', [p[:, 0:128]], [dh[:, 0:128], cst[:, 1 + d, :]], start=True, stop=True)
                        sg_ = fr()
                        k.op('dve', 'scalar_tensor_tensor', [sg_[:, 0:128]],
                             [p[:, 0:128], acs[:, gc, d * 16 + h_:d * 16 + h_ + 1], cst[:, 3 + d, :]], op0=ALU.subtract, op1=ALU.add)
                        k.op('act', 'activation', [sg_[:, 0:128]], [sg_[:, 0:128], AF.Exp])
                        G = br()
                        k.op('dve', 'tensor_tensor', [G[:, 0:128]], [sg_[:, 0:128], sc_t[:, g * 128:(g + 1) * 128], ALU.mult])
                        et = fr()
                        k.op('act', 'activation', [et[:, 0:128]], [p[:, 0:128], AF.Exp])
                        k.op('dve', 'tensor_tensor', [G[:, 128:256]], [et[:, 0:128], V(A1, xbv[:, 10 + g, :]), ALU.mult])
                        xd_ = xdf if d == 0 else xdb
                        hp_ = hpf if d == 0 else hpb
                        k.op('pe', 'matmul', [yslice], [G[:, 0:128], V(xd_[0], xd_[1][:, h_ * 64:(h_ + 1) * 64])], start=(d == 0), stop=False)
                        k.op('pe', 'matmul', [yslice], [G[:, 128:256], V(hp_[0], hp_[1][:, h_ * 64:(h_ + 1) * 64])], start=False, stop=False)
                    k.op('pe', 'matmul', [yslice], [identb[:], V(xdd[0], xdd[1][:, h_ * 64:(h_ + 1) * 64])], start=False, stop=True)
                if KSTOP <= 6:
                    continue
                state_upd(hf, 0)
                for g in range(2):
                    k.op('dve', 'tensor_tensor', [ya[g][:]], [V(zsv[0], zsv[1][:, g * 512:(g + 1) * 512]), YP[g][:], ALU.mult])
                    t = fr()
                    k.op('act', 'activation', [t[:]], [ya[g][:], AF.Square])
                    k.op('dve', 'reduce_sum', [ssq[:, g:g + 1]], [t[:]], axis=mybir.AxisListType.X)
                k.op('dve', 'tensor_tensor', [ssq[:, 2:3]], [ssq[:, 0:1], ssq[:, 1:2], ALU.add])
                k.op('act', 'activation', [ssq[:, 2:3]], [ssq[:, 2:3], AF.Ln], bias=EPS, scale=1.0 / D)
                k.op('act', 'activation', [ssq[:, 2:3]], [ssq[:, 2:3], AF.Exp], scale=-0.5)
                yab = br()
                for g in range(2):
                    k.op('dve', 'tensor_scalar', [yab[:, g * 512:(g + 1) * 512]], [ya[g][:], ssq[:, 2:3], None, ALU.mult])
                p = pb()
                for cc in range(8):
                    k.op('pe', 'transpose', [p[:, cc * 128:(cc + 1) * 128]], [yab[:, cc * 128:(cc + 1) * 128], identb[:]])
                for cc in range(8):
                    k.op('act', 'activation', [V(A2, ycv[:, cc, :])], [p[:, cc * 128:(cc + 1) * 128], AF.Identity], scale=vA(RA_SG + j, cc))
                if KSTOP <= 7:
                    continue
                col0 = s_i * SW + c * 128
                k.dma('sp', V(A2, pinv), raw_d[:, 12:16, col0:col0 + 144])
                ic = fr()
                icv = ic.h[:, :].rearrange("p (g w) -> p g w", g=4)
                k.dma('sp', V(ic, icv), (ics_d if sample else icp_d)[:, :, (c * 128):(c * 128) + 128])
                pob = br()
                pobv = pob.h[:, 0:512].rearrange("p (g w) -> p g w", g=4)
                for gi, win_ in enumerate((2, 4, 8, 16)):
                    S_ = fr()
                    U = S_[:, 0:144]
                    k.op('dve', 'tensor_copy', [U], [V(A2, pinv[:, gi, :])])
                    src = (S_, 0)
                    w_ = 1; cur = 0; n_ = 144
                    regs = [144, 288]
                    while w_ < win_:
                        n2 = n_ - w_
                        dst = regs[0] if cur != regs[0] else regs[1]
                        k.op('dve', 'tensor_tensor', [S_[:, dst:dst + n2]], [S_[:, cur:cur + n2], S_[:, cur + w_:cur + w_ + n2], ALU.add])
                        cur = dst; n_ = n2; w_ *= 2
                    st_ = 8 - win_ // 2
                    k.op('dve', 'tensor_tensor', [S_[:, 432:432 + 0] if False else S_[:, cur + st_:cur + st_ + 128]],
                         [S_[:, cur + st_:cur + st_ + 128], V(ic, icv[:, gi, :]), ALU.mult])
                    k.op('dve', 'tensor_tensor', [V(pob, pobv[:, gi, :])], [S_[:, cur + st_:cur + st_ + 128], S_[:, 8:136], ALU.subtract])
                for gi in range(4):
                    p = ps()
                    k.op('pe', 'matmul', [p[:, 0:128]], [pw[:, gi * 128:(gi + 1) * 128], V(pob, pobv[:, gi, :])], start=True, stop=True)
                    k.op('act', 'activation', [V(A2, ycv[:, 8 + gi, :])], [p[:, 0:128], AF.Identity], scale=vA(RA_PS + j, gi))
                if KSTOP <= 8:
                    continue
                for oc in range(8):
                    p = ps()
                    for kc in range(12):
                        wq = WO[kc // 4]
                        wqv = wq.h[:, :].rearrange("p (kc n) -> p kc n", kc=4)
                        k.op('pe', 'matmul', [p[:, 0:128]], [V(wq, wqv[:, kc % 4, oc * 128:(oc + 1) * 128]), V(A2, ycv[:, kc, :])],
                             start=(kc == 0), stop=(kc == 11))
                    k.op('dve', 'scalar_tensor_tensor', [x[:, oc, tg:tg + 128]],
                         [p[:, 0:128], sm[:, oc, 3:4], x[:, oc, tg:tg + 128]], op0=ALU.mult, op1=ALU.add)
            if not sample:
                store_state(hf, seq0 + s_i, 0)

    def odd_mixer(i, j, cond, L, nseg, sample):
        TN = L * nseg; SW = L + 32
        win = od_w_in[j].rearrange("(kc p) n -> p kc n", p=128)
        wov = od_w_out[j].rearrange("(kc p) n -> p kc n", p=128)
        k.op('dve', 'tensor_copy', [sm[:, :, 3]], [mvec(i, 5, cond)])
        stg = xr()
        k.dma('sp', V(stg, stg.h[:, :].rearrange("p (h s) -> p h s", h=8)), od_sp_w[j].rearrange("h t s -> t h s"))
        for h_ in range(8):
            p = ps()
            k.op('pe', 'transpose', [p[:, 0:128]], [stg[:, h_ * 128:(h_ + 1) * 128], cst[:, 0, :]])
            k.op('act', 'copy', [spw[:, h_, :]], [p[:, 0:128]])
        k.dma('sp', V(spbb, spbb.h[:, :].unsqueeze(1)), spb_d[j, 0:1, :].partition_broadcast(128))
        for b_ in range(2):
            wa = wr(); wb = wr()
            wav = wa.h[:, :].rearrange("p (kc n) -> p kc n", kc=8)
            wbv = wb.h[:, :].rearrange("p (kc n) -> p kc n", kc=8)
            k.dma('pool', V(wa, wav), win[:, :, 2048 + b_ * 512:2048 + (b_ + 1) * 512])
            k.dma('pool', V(wb, wbv), win[:, :, 3072 + b_ * 512:3072 + (b_ + 1) * 512])
            for t0 in range(0, TN, 512):
                for q in range(4):
                    pa = ps(); pg = ps()
                    for kc in range(8):
                        k.op('pe', 'matmul', [pa[:]], [V(wa, wav[:, kc, q * 128:(q + 1) * 128]), hn[:, kc, t0:t0 + 512]], start=(kc == 0), stop=(kc == 7))
                    for kc in range(8):
                        k.op('pe', 'matmul', [pg[:]], [V(wb, wbv[:, kc, q * 128:(q + 1) * 128]), hn[:, kc, t0:t0 + 512]], start=(kc == 0), stop=(kc == 7))
                    sg_ = fr()
                    k.op('act', 'activation', [sg_[:]], [pg[:], AF.Sigmoid])
                    st = br()
                    k.op('dve', 'tensor_tensor', [st[:, 0:512]], [sg_[:], pa[:], ALU.mult])
                    store_padded(glu_d, b_ * 4 + q, st[:, 0:512], t0, L, 16)
        if sample:
            halo_exchange(glu_d, 8, 16, L)
        else:
            zero_pads(glu_d, 8, 16, L, nseg)
        SA, SB, SC, SD = WR[0], WR[1], WR[2], WR[3]
        sav = SA.h[:, :].rearrange("p (c n) -> p c n", c=8)
        sbv = SB.h[:, :].rearrange("p (c n) -> p c n", c=8)
        scv = SC.h[:, :].rearrange("p (c n) -> p c n", c=8)
        sdv = SD.h[:, :].rearrange("p (c n) -> p c n", c=8)
        wslots = [WR[4], WR[5]]
        wctr = [0]

        def wslot():
            w_ = wslots[wctr[0] % 2]; wctr[0] += 1
            return w_

        mean_t, rs_t = ya[0], ya[1]

        def ln_stats(S_, sv_):
            p1 = ps()
            for c in range(8):
                k.op('pe', 'matmul', [p1[:]], [onesb[:], V(S_, sv_[:, c, :])], start=(c == 0), stop=(c == 7))
            for c in range(8):
                k.op('act', 'activation', [V(SC, scv[:, c, :])], [V(S_, sv_[:, c, :]), AF.Square])
            p2 = ps()
            for c in range(8):
                k.op('pe', 'matmul', [p2[:]], [onesb[:], V(SC, scv[:, c, :])], start=(c == 0), stop=(c == 7))
            k.op('dve', 'tensor_scalar', [mean_t[:]], [p1[:], 1.0 / D, None, ALU.mult])
            t = fr()
            k.op('dve', 'tensor_tensor', [t[:]], [mean_t[:], mean_t[:], ALU.mult])
            k.op('dve', 'scalar_tensor_tensor', [rs_t[:]], [p2[:], 1.0 / D, t[:]], op0=ALU.mult, op1=ALU.subtract)
            k.op('act', 'activation', [rs_t[:]], [rs_t[:], AF.Ln], bias=EPS, scale=1.0)
            k.op('act', 'activation', [rs_t[:]], [rs_t[:], AF.Exp], scale=-0.5)

        def ln_apply(S_, sv_, func, grow, brow, bank):
            for c in range(8):
                t = fr()
                k.op('dve', 'tensor_tensor', [t[:]], [V(S_, sv_[:, c, :]), mean_t[:], ALU.subtract])
                k.op('dve', 'tensor_tensor', [t[:]], [t[:], rs_t[:], ALU.mult])
                k.op('act', 'activation', [V(S_, sv_[:, c, :])], [t[:], func], scale=bank(grow, c), bias=bank(brow, c))

        for t0 in range(0, TN, 512):
            for (S_, sv_, colbase) in ((SA, sav, 0), (SB, sbv, 1024)):
                for blk in range(2):
                    w_ = wslot()
                    wv_ = w_.h[:, :].rearrange("p (kc n) -> p kc n", kc=8)
                    k.dma('pool', V(w_, wv_), win[:, :, colbase + blk * 512:colbase + (blk + 1) * 512])
                    for q in range(4):
                        p = ps()
                        for kc in range(8):
                            k.op('pe', 'matmul', [p[:]], [V(w_, wv_[:, kc, q * 128:(q + 1) * 128]), hn[:, kc, t0:t0 + 512]], start=(kc == 0), stop=(kc == 7))
                        k.op('act', 'activation', [V(S_, sv_[:, blk * 4 + q, :])], [p[:], AF.Gelu_apprx_tanh])
            ln_stats(SB, sbv)
            ln_apply(SB, sbv, AF.Identity, RA_VG + j, RA_VB + j, vA)
            for cq in range(4):
                cols = slice(cq * 128, (cq + 1) * 128)
                p = pb()
                for h_ in range(8):
                    k.op('pe', 'transpose', [p[:, h_ * 128:(h_ + 1) * 128]], [V(SB, sbv[:, h_, cols]), identb[:]])
                vt = br()
                k.op('act', 'copy', [vt[:]], [p[:]])
                for h_ in range(8):
                    p2 = ps()
                    k.op('pe', 'matmul', [p2[:, 0:128]], [vt[:, h_ * 128:(h_ + 1) * 128], spw[:, h_, :]], start=True, stop=True)
                    t = fr()
                    k.op('dve', 'tensor_tensor', [t[:, 0:128]], [p2[:, 0:128], spbb[:, h_ * 128:(h_ + 1) * 128], ALU.add])
                    k.op('dve', 'tensor_tensor', [V(SA, sav[:, h_, cols])], [V(SA, sav[:, h_, cols]), t[:, 0:128], ALU.mult])
            tw = min(512, L)
            for sub in range(512 // tw):
                tt = t0 + sub * tw
                s_i = tt // L; lt = tt % L
                for half in range(2):
                    gv = SC.h[:, 0:4 * (tw + 30)].rearrange("p (c w) -> p c w", c=4)
                    col0 = s_i * SW + 16 + lt - 15
                    k.dma('sp', V(SC, gv), glu_d[:, half * 4:(half + 1) * 4, col0:col0 + tw + 30])
                    for c4 in range(4):
                        c = half * 4 + c4
                        acc = fr()
                        k.op('dve', 'tensor_scalar', [acc[:, 0:tw]], [V(SC, gv[:, c4, 0:tw]), vB(RB_DW + j * 31, c), None, ALU.mult])
                        for kk in range(1, 31):
                            k.op('dve', 'scalar_tensor_tensor', [acc[:, 0:tw]],
                                 [V(SC, gv[:, c4, kk:kk + tw]), vB(RB_DW + j * 31 + kk, c), acc[:, 0:tw]], op0=ALU.mult, op1=ALU.add)
                        k.op('act', 'activation', [V(SD, sdv[:, c, sub * tw:(sub + 1) * tw])], [acc[:, 0:tw], AF.Identity],
                             bias=vB(RB_DB + j, c), scale=1.0)
            ln_stats(SD, sdv)
            ln_apply(SD, sdv, AF.Silu, RB_CG + j, RB_CBB + j, vB)
            for ob in range(4):
                w_ = wslot()
                wv_ = w_.h[:, :].rearrange("p (kc n) -> p kc n", kc=16)
                k.dma('pool', V(w_, wv_), wov[:, :, ob * 256:(ob + 1) * 256])
                for o2 in range(2):
                    oc = ob * 2 + o2
                    p = ps()
                    for kc in range(16):
                        rhs = V(SA, sav[:, kc, :]) if kc < 8 else V(SD, sdv[:, kc - 8, :])
                        k.op('pe', 'matmul', [p[:]], [V(w_, wv_[:, kc, o2 * 128:(o2 + 1) * 128]), rhs], start=(kc == 0), stop=(kc == 15))
                    k.op('dve', 'scalar_tensor_tensor', [x[:, oc, t0:t0 + 512]],
                         [p[:], sm[:, oc, 3:4], x[:, oc, t0:t0 + 512]], op0=ALU.mult, op1=ALU.add)

    for (src_d, dst_d, TN, addpos, cond) in ((xp_d, yp_d, TP, False, 0), (xs_d, ys_d, TS, True, 1)):
        sample = (cond == 1)
        if ('S' if sample else 'P') not in KPASS:
            continue
        L, nseg = (TS, 1) if sample else (256, 4)
        load_x(src_d, TN, addpos)
        for i in range(depth):
            normmod(i, 0, cond, TN)
            ffn(i, 0, cond, TN)
            normmod(i, 1, cond, TN)
            if i % 2 == 0:
                even_mixer(i, i // 2, cond, L, nseg, sample, 0)
            else:
                odd_mixer(i, i // 2, cond, L, nseg, sample)
            normmod(i, 2, cond, TN)
            ffn(i, 1, cond, TN)
        final_store(dst_d, TN)
    k.finish('sp')
    env = dict(locals())
    return env


def _host_inputs(inp, c):
    f = np.float32
    b = c // 2; half = c % 2
    d = {}
    d["xp"] = np.ascontiguousarray(inp["x_prompt"][4 * c:4 * c + 4].reshape(TP, D)).astype(f)
    d["xs"] = np.ascontiguousarray(inp["x_sample"][b, half * TS:(half + 1) * TS]).astype(f)
    st = inp["state_ssd"][b]
    d["st"] = np.ascontiguousarray(st.reshape(2, 2, NH * HP, NST)).astype(f)
    rowsA = []
    rowsA += [inp["norm_g"][i, kk] for i in range(DEPTH) for kk in range(3)]
    rowsA += [inp["final_norm_g"]]
    rowsA += [inp["b_mod"][i].reshape(9, D)[kk] for i in range(DEPTH) for kk in range(9)]
    def two(v):
        o = np.zeros((2, D), f); o.reshape(-1)[:v.shape[0]] = v; return [o[0], o[1]]
    for j in range(2):
        for kk in range(5):
            rowsA += two(inp["ev_conv_w"][j, kk])
    for j in range(2):
        rowsA += two(inp["ev_conv_b"][j])
    for j in range(2):
        o = np.zeros((D,), f); o[:512] = inp["ev_pool_scale"][j]; rowsA.append(o)
    rowsA += [inp["c_ctx"], inp["c"][b]]
    rowsA += [inp["ev_ssd_norm_g"][j] for j in range(2)] + [inp["od_v_ln_g"][j] for j in range(2)] + [inp["od_v_ln_b"][j] for j in range(2)]
    d["bankA"] = np.stack(rowsA).astype(f)
    assert d["bankA"].shape == (NRA, D)
    rowsB = [inp["od_dw_w"][j, kk] for j in range(2) for kk in range(31)]
    rowsB += [inp["od_dw_b"][j] for j in range(2)] + [inp["od_cn_g"][j] for j in range(2)] + [inp["od_cn_b"][j] for j in range(2)]
    d["bankB"] = np.stack(rowsB).astype(f)
    i_ = np.arange(128)
    cst = np.zeros((128, 5, 128), f)
    cst[:, 0] = np.eye(128)
    cst[:, 1] = (i_[:, None] <= i_[None, :])
    cst[:, 2] = (i_[:, None] >= i_[None, :])
    cst[:, 3] = np.where(i_[:, None] <= i_[None, :], 0.0, NEG)
    cst[:, 4] = np.where(i_[:, None] >= i_[None, :], 0.0, NEG)
    d["cst"] = cst
    d["eye16"] = np.eye(16, dtype=f)
    mab = np.zeros((128, 2), f); mab[:, half] = 1.0
    d["mab"] = mab
    def invcnt(tglob, Ltot):
        o = np.zeros((4, tglob.shape[0]), f)
        for gi, w_ in enumerate((2, 4, 8, 16)):
            lo = np.clip(tglob - w_ // 2, 0, Ltot); hi = np.clip(tglob - w_ // 2 + w_, 0, Ltot)
            o[gi] = 1.0 / (hi - lo).astype(f)
        return np.ascontiguousarray(np.broadcast_to(o[None], (128,) + o.shape)).astype(f)
    d["icp"] = invcnt(np.arange(256), 256)
    d["ics"] = invcnt(half * TS + np.arange(TS), 2 * TS)
    quarter = D // 4
    freqs = np.exp(-math.log(10000.0) * np.arange(quarter, dtype=f) / quarter).astype(f)
    tg = half * TS + np.arange(TS)
    row = (tg // 64).astype(f); col = (tg % 64).astype(f)
    ar = row[:, None] * freqs; ac = col[:, None] * freqs
    d["pos"] = np.concatenate([np.sin(ar), np.cos(ar), np.sin(ac), np.cos(ac)], -1).astype(f)
    rws = np.zeros((2, 3, 32), f)
    for j in range(2):
        rws[j, 0] = inp["ev_dt_bias"][j].reshape(32); rws[j, 1] = inp["ev_a_log"][j].reshape(32)
        rws[j, 2, :16] = inp["ev_d_skip"][j]
    d["rows"] = rws
    d["spb"] = np.ascontiguousarray(inp["od_sp_b"].reshape(2, 1, D).repeat(2, 1)).astype(f)
    for name in ("w_mod", "ffn_w_gate", "ffn_w_up", "ffn_w_down", "ev_w_in", "ev_pool_w", "ev_w_out",
                 "od_w_in", "od_sp_w", "od_w_out"):
        d[name] = np.ascontiguousarray(inp[name]).astype(f)
    return d


_CACHE = {}


def kernel(**inputs):
    inputs = {k_: np.asarray(v) for k_, v in inputs.items()}
    if "nc" not in _CACHE:
        _CACHE["nc"] = build()["nc"]
    nc = _CACHE["nc"]
    in_maps = [_host_inputs(inputs, c) for c in range(8)]
    res = run_bass_kernel_spmd(nc, in_maps, core_ids=list(range(8)))
    yp = np.stack([res.results[c]["yp"] for c in range(8)]).reshape(32, 256, D).astype(np.float32)
    ys = np.stack([res.results[c]["ys"] for c in range(8)]).reshape(4, 4096, D).astype(np.float32)
    ns = np.stack([res.results[c]["ns"] for c in range(8)]).reshape(32, 2, 2, NH, HP, NST).astype(np.float32)
    return yp, ys, ns
```

```python
import math
import os
import numpy as np
import concourse.bass as bass
import concourse.mybir as mybir
from concourse.bass_utils import run_bass_kernel_spmd

F32 = mybir.dt.float32
BF16 = mybir.dt.bfloat16
AF = mybir.ActivationFunctionType
ALU = mybir.AluOpType

D = 1024; KC = 8; DFF = 2816; DEPTH = 4
NH = 16; HP = 64; NST = 128
DXBC = 1536; DIN_E = 3104; DIN_O = 4096
TP = 1024; TS = 2048
EPS = 1e-6
NEG = -30000.0


class T:
    def __init__(self, h, name):
        self.h = h; self.name = name; self.w = None; self.r = []

    def __getitem__(self, idx):
        return (self, self.h[idx])


def V(t, ap):
    return (t, ap)


class K:
    def __init__(self, nc):
        self.nc = nc
        self.eng = {'pe': nc.tensor, 'act': nc.scalar, 'dve': nc.vector, 'pool': nc.gpsimd, 'sp': nc.sync}
        self.sems = {}; self.cnt = {}
        self.known = {e: {} for e in self.eng}
        self._cms = []
        for e in self.eng:
            self._newsem(e)
        self.ninst = 0; self.nwait = 0; self.ndma = 0
        self.snap = {e: {} for e in self.eng}

    def _newsem(self, key):
        cm = self.nc.semaphore(str(key).replace(' ', ''))
        h = cm.__enter__(); self._cms.append(cm)
        self.sems[key] = h; self.cnt[key] = 0
        return h

    def _enter(self, cm):
        h = cm.__enter__(); self._cms.append(cm); return h

    def sbuf(self, name, shape, dt):
        return T(self._enter(self.nc.sbuf_tensor(name, list(shape), dt)), name)

    def psum(self, name, shape, dt=F32):
        return T(self._enter(self.nc.psum_tensor(name, list(shape), dt)), name)

    def dram(self, name, shape, dt):
        return T(self.nc.dram_tensor(name, list(shape), dt), name)

    def _wait(self, e, tok):
        key, val = tok
        if key[0] == 'd':
            val = max(val, self.cnt[key])
        if e == 'pe' and key == 'pe':
            return
        if self.known[e].get(key, 0) >= val:
            return
        self.eng[e].wait_ge(self.sems[key], val)
        self.known[e][key] = val
        self.nwait += 1
        if key in self.snap and val in self.snap[key]:
            ke = self.known[e]
            for k2, v2 in self.snap[key][val].items():
                if k2 != e and ke.get(k2, 0) < v2:
                    ke[k2] = v2

    def _deps(self, e, reads, writes):
        for t in reads:
            if t.w is not None:
                self._wait(e, t.w)
        for t in writes:
            if t.w is not None:
                self._wait(e, t.w)
            for tok in t.r:
                self._wait(e, tok)

    def _mark(self, tok, reads, writes):
        for t in reads:
            t.r.append(tok)
            if len(t.r) > 24:
                best = {}
                for kk, vv in t.r:
                    best[kk] = max(best.get(kk, 0), vv)
                t.r = list(best.items())
        for t in writes:
            t.w = tok; t.r = []

    @staticmethod
    def _split(args):
        ts, aps = [], []
        for a in args:
            if isinstance(a, tuple) and len(a) == 2 and isinstance(a[0], T):
                ts.append(a[0]); aps.append(a[1])
            else:
                aps.append(a)
        return ts, aps

    def op(self, e, fn, outs, ins, **kw):
        wt, oaps = self._split(outs)
        rt, iaps = self._split(ins)
        kw2 = {}
        for k_, v in kw.items():
            if isinstance(v, tuple) and len(v) == 2 and isinstance(v[0], T):
                if k_ == 'accum_out':
                    wt.append(v[0])
                else:
                    rt.append(v[0])
                kw2[k_] = v[1]
            else:
                kw2[k_] = v
        self._deps(e, rt, wt)
        inst = getattr(self.eng[e], fn)(*oaps, *iaps, **kw2)
        self.cnt[e] += 1
        inst.then_inc(self.sems[e], 1)
        self.snap[e][self.cnt[e]] = dict(self.known[e])
        self._mark((e, self.cnt[e]), rt, wt)
        self.ninst += 1
        return inst

    def dma(self, q, out, in_, **kw):
        wt, oaps = self._split([out])
        rt, iaps = self._split([in_])
        self._deps(q, rt, wt)
        t = (wt + rt)[0]
        semkey = ('d', t.name, q)
        if semkey not in self.sems:
            self._newsem(semkey)
        inst = self.eng[q].dma_start(out=oaps[0], in_=iaps[0], **kw)
        self.cnt[semkey] += 16
        inst.then_inc(self.sems[semkey], 16)
        self._mark((semkey, self.cnt[semkey]), rt, wt)
        self.ndma += 1
        return inst

    def collective(self, cin, cout):
        self._deps('pool', [cin], [cout])
        inst = self.nc.gpsimd.collective_compute(
            "AllGather", ALU.bypass, replica_groups=[[0, 1], [2, 3], [4, 5], [6, 7]],
            ins=[cin.h.ap()], outs=[cout.h.ap()])
        sk = ('c', 'cc')
        if sk not in self.sems:
            self._newsem(sk)
        self.cnt[sk] += 1
        inst.then_inc(self.sems[sk])
        self._mark((sk, self.cnt[sk]), [cin], [cout])

    def finish(self, e='sp'):
        for key in self.sems:
            if self.cnt[key] > 0 and self.known[e].get(key, 0) < self.cnt[key]:
                self.eng[e].wait_ge(self.sems[key], self.cnt[key])
                self.known[e][key] = self.cnt[key]

RA_NG = 0; RA_FN = 12; RA_BM = 13; RA_CW = 49; RA_CB = 69; RA_PS = 73; RA_CD = 75; RA_SG = 77; RA_VG = 79; RA_VB = 81; NRA = 83
RB_DW = 0; RB_DB = 62; RB_CG = 64; RB_CBB = 66; NRB = 68


def build(depth=DEPTH):
    KSTOP = int(os.environ.get('KSTOP', '9')); KPASS = os.environ.get('KPASS', 'PS'); KSUB = int(os.environ.get('KSUB', '9'))
    nc = bass.Bass("TRN2", target_bir_lowering=False)
    k = K(nc)

    def din(name, shape):
        return nc.dram_tensor(name, list(shape), F32, kind="ExternalInput").ap()

    def dout(name, shape):
        return nc.dram_tensor(name, list(shape), F32, kind="ExternalOutput").ap()

    xp_d = din("xp", [TP, D]); xs_d = din("xs", [TS, D]); pos_d = din("pos", [TS, D])
    st_d = din("st", [2, 2, NH * HP, NST])
    bankA_d = din("bankA", [NRA, D]); bankB_d = din("bankB", [NRB, D])
    cst_d = din("cst", [128, 5, 128]); eye_d = din("eye16", [16, 16]); mab_d = din("mab", [128, 2])
    icp_d = din("icp", [128, 4, 256]); ics_d = din("ics", [128, 4, TS])
    spb_d = din("spb", [2, 2, D]); rows_d = din("rows", [2, 3, 32])
    z_dummy = None
    w_mod = din("w_mod", [DEPTH, D, 9 * D])
    wg_d = din("ffn_w_gate", [DEPTH, 2, D, DFF]); wu_d = din("ffn_w_up", [DEPTH, 2, D, DFF])
    wd_d = din("ffn_w_down", [DEPTH, 2, DFF, D])
    ev_w_in = din("ev_w_in", [2, D, DIN_E]); ev_pool_w = din("ev_pool_w", [2, 4, 128, 128])
    ev_w_out = din("ev_w_out", [2, 1536, D])
    od_w_in = din("od_w_in", [2, D, DIN_O]); od_sp_w = din("od_sp_w", [2, 8, 128, 128])
    od_w_out = din("od_w_out", [2, 2048, D])
    yp_d = dout("yp", [TP, D]); ys_d = dout("ys", [TS, D]); ns_d = dout("ns", [4, 2, 2, NH * HP, NST])

    x = k.sbuf("x", [128, 8, TS], F32)
    hn = k.sbuf("hn", [128, 8, TS], BF16)
    WR = [k.sbuf(f"wr{i}", [128, 4096], BF16) for i in range(6)]
    FR = [k.sbuf(f"fr{i}", [128, 512], F32) for i in range(5)]
    rst = k.sbuf("rst", [128, 512], F32)
    BR = [k.sbuf(f"br{i}", [128, 1024], BF16) for i in range(5)]
    XR = [k.sbuf(f"xr{i}", [128, 1024], F32) for i in range(2)]
    PS = [k.psum(f"ps{i}", [128, 512], F32) for i in range(4)]
    YP = [k.psum(f"yp{i}", [128, 512], F32) for i in range(2)]
    PB = [k.psum(f"pb{i}", [128, 1024], BF16) for i in range(2)]
    ctr = {}

    def ring(name, lst):
        i = ctr.get(name, 0); ctr[name] = i + 1
        return lst[i % len(lst)]

    wr = lambda: ring('wr', WR)
    fr = lambda: ring('fr', FR)
    br = lambda: ring('br', BR)
    xr = lambda: ring('xr', XR)
    ps = lambda: ring('ps', PS)
    pb = lambda: ring('pb', PB)

    cst = k.sbuf("cst_sb", [128, 5, 128], F32)
    identb = k.sbuf("identb", [128, 128], BF16)
    onesb = k.sbuf("onesb", [128, 128], BF16)
    onesf = k.sbuf("onesf", [128, 128], F32)
    eye16 = k.sbuf("eye16_sb", [16, 16], F32)
    mab = k.sbuf("mab_sb", [128, 2], F32)
    vta = k.sbuf("vta", [128, 8, NRA], F32)
    vtb = k.sbuf("vtb", [128, 8, NRB], F32)
    modv = k.sbuf("modv", [128, DEPTH, 72, 2], F32)
    sct = k.sbuf("sct", [128, 8, 2], BF16)
    rows = k.sbuf("rows_sb", [128, 3, 32], F32)
    sm = k.sbuf("sm", [128, 8, 8], F32)

    k.dma('sp', cst[:], cst_d)
    k.dma('sp', eye16[:], eye_d)
    k.dma('sp', mab[:], mab_d)
    ident = cst[:, 0, :]
    k.op('dve', 'tensor_copy', [identb[:]], [cst[:, 0, :]])
    k.op('dve', 'memset', [onesb[:]], [1.0])
    k.op('dve', 'memset', [onesf[:]], [1.0])

    def load_bank(bank_d, n, vt):
        bk = xr()
        k.dma('sp', bk[0:n, :], bank_d)
        for c in range(8):
            p = ps()
            k.op('pe', 'transpose', [p[:, 0:n]], [bk[0:n, c * 128:(c + 1) * 128], cst[0:n, 0, 0:n]])
            k.op('dve', 'tensor_copy', [vt[:, c, :]], [p[:, 0:n]])

    load_bank(bankA_d, NRA, vta)
    load_bank(bankB_d, NRB, vtb)

    def vA(r, c):
        return vta[:, c, r:r + 1]

    def vB(r, c):
        return vtb[:, c, r:r + 1]

    k.op('act', 'activation', [sct[:]], [vta[:, :, RA_CD:RA_CD + 2], AF.Silu])
    for i in range(depth):
        wv = w_mod[i].rearrange("(kc p) n -> p kc n", p=128)
        for blk in range(18):
            s = wr()
            sv = V(s, s.h[:, :].rearrange("p (kc n) -> p kc n", kc=8))
            k.dma('pool', sv, wv[:, :, blk * 512:(blk + 1) * 512])
            p = ps()
            for q in range(4):
                for kc in range(8):
                    k.op('pe', 'matmul', [p[:, q * 2:q * 2 + 2]],
                         [V(s, sv[1][:, kc, q * 128:(q + 1) * 128]), sct[:, kc, :]], start=(kc == 0), stop=(kc == 7))
            fc0 = blk * 4
            kk = fc0 // 8; c0 = fc0 % 8
            bias = V(vta, vta.h[:, c0:c0 + 4, RA_BM + i * 9 + kk].unsqueeze(2).to_broadcast([128, 4, 2]))
            k.op('dve', 'tensor_tensor', [modv[:, i, fc0:fc0 + 4, :]],
                 [V(p, p.h[:, 0:8].rearrange("p (a b) -> p a b", b=2)), bias, ALU.add])

    def mvec(i, kk, cond):
        return V(modv, modv.h[:, i, kk * 8:(kk + 1) * 8, cond])

    def rstd_tile(src_fn, tw, scale=1.0 / D):
        sq = wr()
        sqv = V(sq, sq.h[:, :].rearrange("p (c n) -> p c n", c=8))
        for c in range(8):
            k.op('act', 'activation', [V(sq, sqv[1][:, c, 0:tw])], [src_fn(c), AF.Square])
        p = ps()
        for c in range(8):
            k.op('pe', 'matmul', [p[:, 0:tw]], [onesb[:], V(sq, sqv[1][:, c, 0:tw])], start=(c == 0), stop=(c == 7))
        r = rst
        k.op('act', 'activation', [r[:, 0:tw]], [p[:, 0:tw], AF.Ln], bias=EPS, scale=scale)
        k.op('act', 'activation', [r[:, 0:tw]], [r[:, 0:tw], AF.Exp], scale=-0.5)
        return r

    def normmod(i, which, cond, TN):
        g_ = V(vta, vta.h[:, :, RA_NG + i * 3 + which])
        k.op('dve', 'scalar_tensor_tensor', [sm[:, :, 0]], [mvec(i, 3 * which + 1, cond), 1.0, g_], op0=ALU.add, op1=ALU.mult)
        k.op('dve', 'tensor_copy', [sm[:, :, 1]], [mvec(i, 3 * which, cond)])
        for t0 in range(0, TN, 512):
            r = rstd_tile(lambda c: x[:, c, t0:t0 + 512], 512)
            for c in range(8):
                tmp = fr()
                k.op('dve', 'tensor_tensor', [tmp[:]], [x[:, c, t0:t0 + 512], r[:], ALU.mult])
                k.op('act', 'activation', [hn[:, c, t0:t0 + 512]], [tmp[:], AF.Identity],
                     scale=sm[:, c, 0:1], bias=sm[:, c, 1:2])

    def ffn(i, f, cond, TN):
        kk = 2 if f == 0 else 8
        k.op('dve', 'tensor_scalar', [sm[:, :, 2]], [mvec(i, kk, cond), 0.5, None, ALU.mult])
        blocks = [(c0, min(512, DFF - c0)) for c0 in range(0, DFF, 512)]
        wgv = wg_d[i, f].rearrange("(kc p) n -> p kc n", p=128)
        wuv = wu_d[i, f].rearrange("(kc p) n -> p kc n", p=128)
        wdv = wd_d[i, f].rearrange("(j p) n -> p j n", p=128)

        def load(bi):
            c0, cw = blocks[bi]
            sg, su, sd = wr(), wr(), wr()
            sgv = sg.h[:, 0:8 * cw].rearrange("p (kc n) -> p kc n", kc=8)
            suv = su.h[:, 0:8 * cw].rearrange("p (kc n) -> p kc n", kc=8)
            nj = cw // 128
            sdv = sd.h[:, 0:nj * 1024].rearrange("p (j n) -> p j n", j=nj)
            k.dma('pool', V(sg, sgv), wgv[:, :, c0:c0 + cw])
            k.dma('pool', V(su, suv), wuv[:, :, c0:c0 + cw])
            k.dma('pool', V(sd, sdv), wdv[:, c0 // 128:c0 // 128 + nj, :])
            return (sg, sgv, su, suv, sd, sdv, nj)

        nxt = load(0)
        for bi in range(len(blocks)):
            cur = nxt
            if bi + 1 < len(blocks):
                nxt = load(bi + 1)
            sg, sgv, su, suv, sd, sdv, nj = cur
            for t0 in range(0, TN, 512):
                hc = [br() for _ in range(2)]
                hcv = lambda j: hc[j // 2][:, (j % 2) * 512:(j % 2 + 1) * 512]
                for j in range(nj):
                    pg = ps(); pu = ps()
                    for kc in range(8):
                        k.op('pe', 'matmul', [pg[:]], [V(sg, sgv[:, kc, j * 128:(j + 1) * 128]), hn[:, kc, t0:t0 + 512]],
                             start=(kc == 0), stop=(kc == 7))
                    for kc in range(8):
                        k.op('pe', 'matmul', [pu[:]], [V(su, suv[:, kc, j * 128:(j + 1) * 128]), hn[:, kc, t0:t0 + 512]],
                             start=(kc == 0), stop=(kc == 7))
                    sgt = fr()
                    k.op('act', 'activation', [sgt[:]], [pg[:], AF.Silu])
                    k.op('dve', 'tensor_tensor', [hcv(j)], [sgt[:], pu[:], ALU.mult])
                for oc in range(8):
                    pd = ps()
                    for j in range(nj):
                        k.op('pe', 'matmul', [pd[:]], [V(sd, sdv[:, j, oc * 128:(oc + 1) * 128]), hcv(j)],
                             start=(j == 0), stop=(j == nj - 1))
                    k.op('dve', 'scalar_tensor_tensor', [x[:, oc, t0:t0 + 512]],
                         [pd[:], sm[:, oc, 2:3], x[:, oc, t0:t0 + 512]], op0=ALU.mult, op1=ALU.add)

    def load_x(src_d, TN, addpos):
        for g in range(TN // 128):
            t = xr()
            k.dma('sp', t[:], src_d[g * 128:(g + 1) * 128, :])
            if addpos:
                t2 = xr()
                k.dma('sp', t2[:], pos_d[g * 128:(g + 1) * 128, :])
                k.op('dve', 'tensor_tensor', [t[:]], [t[:], t2[:], ALU.add])
            for h in range(2):
                p = ps()
                for q in range(4):
                    c = h * 4 + q
                    k.op('pe', 'transpose', [p[:, q * 128:(q + 1) * 128]], [t[:, c * 128:(c + 1) * 128], cst[:, 0, :]])
                k.op('act', 'copy', [x[:, h * 4:(h + 1) * 4, g * 128:(g + 1) * 128]],
                     [V(p, p.h[:, :].rearrange("p (q n) -> p q n", q=4))])

    def final_store(dst_d, TN):
        for t0 in range(0, TN, 512):
            r = rstd_tile(lambda c: x[:, c, t0:t0 + 512], 512)
            for g in range(4):
                t = xr()
                for c in range(8):
                    tmp = fr()
                    k.op('dve', 'scalar_tensor_tensor', [tmp[:, 0:128]],
                         [x[:, c, t0 + g * 128:t0 + (g + 1) * 128], vA(RA_FN, c), r[:, g * 128:(g + 1) * 128]],
                         op0=ALU.mult, op1=ALU.mult)
                    p = ps()
                    k.op('pe', 'transpose', [p[:, 0:128]], [tmp[:, 0:128], cst[:, 0, :]])
                    k.op('act', 'copy', [t[:, c * 128:(c + 1) * 128]], [p[:, 0:128]])
                k.dma('sp', dst_d[t0 + g * 128:t0 + (g + 1) * 128, :], t[:])


    WP = TS + 32
    raw_d = k.dram("raw_d", [128, 16, WP], BF16)
    xbc_d = k.dram("xbc_d", [128, 12, TS], BF16)
    hpb_d = k.dram("hpb_d", [16, 128, 1024], BF16)
    z_d = k.dram("z_d", [TS, 1024], BF16)
    glu_d = k.dram("glu_d", [128, 8, WP + 128], BF16)
    dts = k.sbuf("dts", [128, 16, 32], F32)
    acs = k.sbuf("acs", [128, 16, 32], F32)
    pbs = k.sbuf("pbs", [128, 16, 16], F32)
    sA = k.sbuf("sA", [128, 32], F32); sE = k.sbuf("sE", [128, 32], F32); sD = k.sbuf("sD", [128, 32], F32)
    sF = k.sbuf("sF", [128, 32], F32); sC = k.sbuf("sC", [128, 32], F32)
    spw = k.sbuf("spw", [128, 8, 128], BF16)
    spbb = k.sbuf("spbb", [128, 1024], F32)
    zb = k.sbuf("zb", [128, 16, 16], BF16)
    facc = k.sbuf("facc", [128, 1024], F32)
    ya = [k.sbuf("ya0", [128, 512], F32), k.sbuf("ya1", [128, 512], F32)]
    sc_t = k.sbuf("sc_t", [128, 256], BF16)
    pw = k.sbuf("pw", [128, 512], BF16)
    ssq = k.sbuf("ssq", [128, 4], F32)
    pf = k.sbuf("pf", [128, 16], F32); pbr = k.sbuf("pbr", [128, 16], F32)
    k.op('dve', 'memset', [zb[:]], [0.0])
    zsrc = WR[5]
    k.op('dve', 'memset', [zsrc[:]], [0.0])
    for c_ in range(16):
        k.dma('sp', raw_d[:, c_, :], zsrc[:, 0:WP])
    for c_ in range(8):
        k.dma('sp', glu_d[:, c_, :], zsrc[:, 0:WP + 128])
    for c_ in range(12):
        k.dma('sp', xbc_d[:, c_, :], zsrc[:, 0:TS])
    for c_ in range(16):
        k.dma('sp', hpb_d[c_, :, :], zsrc[:, 0:1024])
        k.dma('sp', z_d[c_ * 128:(c_ + 1) * 128, :], zsrc[:, 0:1024])
    exn = [0]

    def exchange(src, F):
        n = exn[0]; exn[0] += 1
        cin = k.dram(f"cin{n}", [128, F], F32); cout = k.dram(f"cout{n}", [256, F], F32)
        k.dma('sp', cin[:, :], src)
        k.collective(cin, cout)
        g = fr()
        k.dma('sp', V(g, g.h[:, 0:2 * F].rearrange("p (r f) -> p r f", r=2)),
              V(cout, cout.h.ap().rearrange("(r p) f -> p r f", p=128)))
        o = fr()
        k.op('dve', 'tensor_scalar', [o[:, 0:F]], [g[:, 0:F], mab[:, 1:2], None, ALU.mult])
        k.op('dve', 'scalar_tensor_tensor', [o[:, 0:F]], [g[:, F:2 * F], mab[:, 0:1], o[:, 0:F]], op0=ALU.mult, op1=ALU.add)
        return o

    def halo_exchange(dram, nchunk, padw, L):
        F = nchunk * 2 * padw
        eb = br()
        ebv = eb.h[:, 0:F].rearrange("p (c w) -> p c w", w=2 * padw)
        k.dma('sp', V(eb, ebv[:, :, 0:padw]), dram[:, 0:nchunk, padw:2 * padw])
        k.dma('sp', V(eb, ebv[:, :, padw:2 * padw]), dram[:, 0:nchunk, L:L + padw])
        ef = fr()
        k.op('dve', 'tensor_copy', [ef[:, 0:F]], [eb[:, 0:F]])
        part = exchange(ef[:, 0:F], F)
        pv = part.h[:, 0:F].rearrange("p (c w) -> p c w", w=2 * padw)
        pads = br()
        pdv = pads.h[:, 0:F].rearrange("p (c w) -> p c w", w=2 * padw)
        k.op('dve', 'tensor_scalar', [V(pads, pdv[:, :, 0:padw])], [V(part, pv[:, :, padw:2 * padw]), mab[:, 1:2], None, ALU.mult])
        k.op('dve', 'tensor_scalar', [V(pads, pdv[:, :, padw:2 * padw])], [V(part, pv[:, :, 0:padw]), mab[:, 0:1], None, ALU.mult])
        k.dma('sp', dram[:, 0:nchunk, 0:padw], V(pads, pdv[:, :, 0:padw]))
        k.dma('sp', dram[:, 0:nchunk, padw + L:2 * padw + L], V(pads, pdv[:, :, padw:2 * padw]))

    def zero_pads(dram, nchunk, padw, L, nseg):
        SW = L + 2 * padw
        for s_i in range(nseg):
            k.dma('sp', dram[:, 0:nchunk, s_i * SW:s_i * SW + padw], zb[:, 0:nchunk, 0:padw])
            k.dma('sp', dram[:, 0:nchunk, s_i * SW + padw + L:(s_i + 1) * SW], zb[:, 0:nchunk, 0:padw])

    def store_padded(dram, ch, st, t0, L, padw):
        SW = L + 2 * padw
        if L >= 512:
            k.dma('sp', dram[:, ch, padw + t0:padw + t0 + 512], st)
        else:
            s0 = t0 // L; ns_ = 512 // L
            dst = dram.h.ap()[:, ch, s0 * SW:(s0 + ns_) * SW].rearrange("p (s w) -> p s w", w=SW)[:, :, padw:padw + L]
            k.dma('sp', V(dram, dst), V(st[0], st[1].rearrange("p (s w) -> p s w", w=L)))

    def proj_fm(wv_cols, nchk, TN, consume):
        s = wr()
        sv = s.h[:, :].rearrange("p (kc n) -> p kc n", kc=8)
        k.dma('pool', V(s, sv[:, :, 0:nchk * 128]), wv_cols)
        for t0 in range(0, TN, 512):
            for q in range(nchk):
                p = ps()
                for kc in range(8):
                    k.op('pe', 'matmul', [p[:]], [V(s, sv[:, kc, q * 128:(q + 1) * 128]), hn[:, kc, t0:t0 + 512]],
                         start=(kc == 0), stop=(kc == 7))
                consume(p, q, t0)

    def bc16(t, ap16):
        return V(t, ap16.unsqueeze(2).to_broadcast([128, 16, HP]))

    def bc8(t, ap8):
        return V(t, ap8.unsqueeze(2).to_broadcast([128, 8, HP]))

    def v3(tv):
        return V(tv[0], tv[1].rearrange("p (h q) -> p h q", q=HP))

    def even_mixer(i, j, cond, L, nseg, sample, seq0):
        TN = L * nseg; nch = L // 128; SW = L + 16
        win = ev_w_in[j].rearrange("(kc p) n -> p kc n", p=128)
        k.dma('sp', V(rows, rows.h[:, :, :].rearrange("p a b -> p (a b)").unsqueeze(1)),
              rows_d.rearrange("j a b -> j (a b)")[j:j + 1, :].partition_broadcast(128))
        k.op('act', 'activation', [rows[:, 1, :]], [rows[:, 1, :], AF.Exp])
        k.op('dve', 'tensor_scalar', [rows[:, 1, :]], [rows[:, 1, :], -1.0, None, ALU.mult])
        k.op('dve', 'tensor_copy', [sm[:, :, 3]], [mvec(i, 5, cond)])

        def raw_consume(cb):
            def f(p, q, t0):
                st = br()
                k.op('act', 'copy', [st[:, 0:512]], [p[:]])
                store_padded(raw_d, cb + q, st[:, 0:512], t0, L, 8)
            return f
        for b_ in range(3):
            proj_fm(win[:, :, D + b_ * 512:D + (b_ + 1) * 512], 4, TN, raw_consume(b_ * 4))
        proj_fm(win[:, :, D + DXBC + 32:D + DXBC + 32 + 512], 4, TN, raw_consume(12))
        if sample:
            halo_exchange(raw_d, 16, 8, L)
        else:
            zero_pads(raw_d, 16, 8, L, nseg)
        for g in range(2):
            s = wr()
            sv = s.h[:, :].rearrange("p (kc n) -> p kc n", kc=8)
            k.dma('pool', V(s, sv), win[:, :, g * 512:(g + 1) * 512])
            for gc in range(TN // 128):
                p = ps()
                for kc in range(8):
                    k.op('pe', 'matmul', [p[:]], [hn[:, kc, gc * 128:(gc + 1) * 128], V(s, sv[:, kc, :])], start=(kc == 0), stop=(kc == 7))
                st = br()
                k.op('act', 'activation', [st[:, 0:512]], [p[:], AF.Silu])
                k.dma('sp', z_d[gc * 128:(gc + 1) * 128, g * 512:(g + 1) * 512], st[:, 0:512])
        s = wr()
        sv = s.h[:, 0:256].rearrange("p (kc n) -> p kc n", kc=8)
        k.dma('pool', V(s, sv), win[:, :, D + DXBC:D + DXBC + 32])
        for gc in range(TN // 128):
            p = ps()
            for kc in range(8):
                k.op('pe', 'matmul', [p[:, 0:32]], [hn[:, kc, gc * 128:(gc + 1) * 128], V(s, sv[:, kc, :])],
                     start=(kc == 0), stop=(kc == 7))
            tmp = fr()
            k.op('dve', 'tensor_tensor', [tmp[:, 0:32]], [p[:, 0:32], rows[:, 0, :], ALU.add])
            k.op('act', 'activation', [tmp[:, 0:32]], [tmp[:, 0:32], AF.Exp])
            k.op('act', 'activation', [dts[:, gc, :]], [tmp[:, 0:32], AF.Ln], bias=1.0, scale=1.0)
        if KSTOP <= 1:
            return
        for s_i in range(nseg):
            for t0 in range(0, L, 512):
                tw = min(512, L - t0)
                for half in range(2):
                    rt = wr()
                    rv = rt.h[:, 0:6 * (tw + 4)].rearrange("p (c w) -> p c w", c=6)
                    col0 = s_i * SW + 8 + t0 - 2
                    k.dma('sp', V(rt, rv), raw_d[:, half * 6:(half + 1) * 6, col0:col0 + tw + 4])
                    for c6 in range(6):
                        c = half * 6 + c6
                        hi = 1 if c >= 8 else 0
                        acc = fr()
                        k.op('dve', 'tensor_scalar', [acc[:, 0:tw]],
                             [V(rt, rv[:, c6, 0:tw]), vA(RA_CW + (j * 5) * 2 + hi, c % 8), None, ALU.mult])
                        for kk in range(1, 5):
                            k.op('dve', 'scalar_tensor_tensor', [acc[:, 0:tw]],
                                 [V(rt, rv[:, c6, kk:kk + tw]), vA(RA_CW + (j * 5 + kk) * 2 + hi, c % 8), acc[:, 0:tw]],
                                 op0=ALU.mult, op1=ALU.add)
                        ob = br()
                        k.op('act', 'activation', [ob[:, 0:tw]], [acc[:, 0:tw], AF.Silu],
                             bias=vA(RA_CB + j * 2 + hi, c % 8), scale=1.0)
                        k.dma('sp', xbc_d[:, c, s_i * L + t0:s_i * L + t0 + tw], ob[:, 0:tw])

        if KSTOP <= 2:
            return
        A0, A1, A2 = WR[0], WR[1], WR[2]
        xs_t = V(A0, A0.h[:, 0:1024]); xdf = V(A0, A0.h[:, 1024:2048]); xdb = V(A0, A0.h[:, 2048:3072]); xdd = V(A0, A0.h[:, 3072:4096])
        xbv = A1.h[:, 0:1536].rearrange("p (c w) -> p c w", c=12)
        bt = V(A1, A1.h[:, 1536:1792]); hpf = V(A1, A1.h[:, 2048:3072]); hpb = V(A1, A1.h[:, 3072:4096])
        ycv = A2.h[:, 0:1536].rearrange("p (c w) -> p c w", c=12)
        pinv = A2.h[:, 2048:2048 + 576].rearrange("p (g w) -> p g w", g=4)
        zsv = V(A2, A2.h[:, 3072:4096])
        hf, hb = XR[0], XR[1]
        WO = [WR[3], WR[4], WR[5]]
        wov = ev_w_out[j].rearrange("(kc p) n -> p kc n", p=128)
        for q in range(3):
            k.dma('pool', V(WO[q], WO[q].h[:, :].rearrange("p (kc n) -> p kc n", kc=4)), wov[:, q * 4:(q + 1) * 4, :])
        k.dma('pool', V(pw, pw.h[:, :].rearrange("p (g n) -> p g n", g=4)), ev_pool_w[j].rearrange("g c d -> c g d"))

        def prep(gc):
            k.dma('sp', V(A1, xbv), xbc_d[:, :, gc * 128:(gc + 1) * 128])
            p = pb()
            for c in range(8):
                k.op('pe', 'transpose', [p[:, c * 128:(c + 1) * 128]], [V(A1, xbv[:, c, :]), identb[:]])
            k.op('act', 'copy', [xs_t], [p[:]])
            p2 = pb()
            for g in range(2):
                k.op('pe', 'transpose', [p2[:, g * 128:(g + 1) * 128]], [V(A1, xbv[:, 8 + g, :]), identb[:]])
            k.op('act', 'copy', [bt], [p2[:, 0:256]])
            k.op('dve', 'tensor_tensor', [sA[:]], [dts[:, gc, :], rows[:, 1, :], ALU.mult])
            p3 = ps()
            k.op('pe', 'matmul', [p3[:, 0:16]], [cst[:, 1, :], sA[:, 0:16]], start=True, stop=True)
            k.op('pe', 'matmul', [p3[:, 16:32]], [cst[:, 2, :], sA[:, 16:32]], start=True, stop=True)
            k.op('pe', 'matmul', [p3[:, 32:64]], [onesf[:], sA[:]], start=True, stop=True)
            k.op('dve', 'tensor_copy', [acs[:, gc, :]], [p3[:, 0:32]])
            k.op('act', 'activation', [sE[:]], [p3[:, 0:32], AF.Exp])
            k.op('dve', 'tensor_tensor', [sD[:]], [p3[:, 32:64], acs[:, gc, :], ALU.subtract])
            k.op('act', 'activation', [sD[:]], [sD[:], AF.Exp])
            k.op('dve', 'tensor_tensor', [sF[:]], [sD[:], dts[:, gc, :], ALU.mult])
            k.op('act', 'activation', [sC[:]], [p3[:, 32:64], AF.Exp])

        def chunk_state(d):
            k.op('dve', 'tensor_tensor', [v3(xdd)], [v3(xs_t), bc16(sF, sF.h[:, d * 16:(d + 1) * 16]), ALU.mult])
            out = []
            for g in range(2):
                p = ps()
                k.op('pe', 'matmul', [p[:]], [V(bt[0], bt[1][:, g * 128:(g + 1) * 128]), V(xdd[0], xdd[1][:, g * 512:(g + 1) * 512])],
                     start=True, stop=True)
                out.append(p)
            return out

        def state_upd(h_t, d):
            pp = chunk_state(d)
            k.op('dve', 'tensor_tensor', [v3(h_t[:, :])], [v3(h_t[:, :]), bc16(sC, sC.h[:, d * 16:(d + 1) * 16]), ALU.mult])
            for g in range(2):
                k.op('dve', 'tensor_tensor', [h_t[:, g * 512:(g + 1) * 512]], [h_t[:, g * 512:(g + 1) * 512], pp[g][:], ALU.add])

        def load_state(h_t, d, mcol):
            if mcol is None:
                k.op('dve', 'memset', [h_t[:]], [0.0])
                return
            for c in range(8):
                t = fr()
                k.dma('sp', t[:, 0:128], st_d[j, d, c * 128:(c + 1) * 128, :])
                p = ps()
                k.op('pe', 'transpose', [p[:, 0:128]], [t[:, 0:128], cst[:, 0, :]])
                k.op('dve', 'tensor_scalar', [h_t[:, c * 128:(c + 1) * 128]], [p[:, 0:128], mab[:, mcol:mcol + 1], None, ALU.mult])

        def store_state(h_t, seq, d):
            for c in range(8):
                p = ps()
                k.op('pe', 'transpose', [p[:, 0:128]], [h_t[:, c * 128:(c + 1) * 128], cst[:, 0, :]])
                t = fr()
                k.op('act', 'copy', [t[:, 0:128]], [p[:, 0:128]])
                k.dma('sp', ns_d[seq, j, d, c * 128:(c + 1) * 128, :], t[:, 0:128])

        for s_i in range(nseg):
            gc0 = s_i * nch
            load_state(hb, 1, 1 if sample else None)
            k.op('dve', 'memset', [pbr[:]], [1.0])
            if sample:
                k.op('dve', 'memset', [facc[:]], [0.0])
                k.op('dve', 'memset', [pf[:]], [1.0])
            for c in reversed(range(nch)):
                gc = gc0 + c
                prep(gc)
                k.op('act', 'copy', [hpb], [hb[:]])
                k.dma('sp', hpb_d[c, :, :], hpb)
                k.op('dve', 'tensor_copy', [pbs[:, c, :]], [pbr[:]])
                k.op('dve', 'tensor_tensor', [pbr[:]], [pbr[:], sC[:, 16:32], ALU.mult])
                state_upd(hb, 1)
                if sample:
                    k.op('dve', 'tensor_tensor', [v3(xdd)], [v3(xs_t), bc16(sF, sF.h[:, 0:16]), ALU.mult])
                    k.op('dve', 'tensor_tensor', [v3(xdd)], [v3(xdd), bc16(pf, pf.h[:, :]), ALU.mult])
                    for g in range(2):
                        p = ps()
                        k.op('pe', 'matmul', [p[:]], [V(bt[0], bt[1][:, g * 128:(g + 1) * 128]), V(xdd[0], xdd[1][:, g * 512:(g + 1) * 512])],
                             start=True, stop=True)
                        k.op('dve', 'tensor_tensor', [facc[:, g * 512:(g + 1) * 512]], [facc[:, g * 512:(g + 1) * 512], p[:], ALU.add])
                    k.op('dve', 'tensor_tensor', [pf[:]], [pf[:], sC[:, 0:16], ALU.mult])
            if not sample:
                store_state(hb, seq0 + s_i, 1)
                load_state(hf, 0, None)
            else:
                load_state(hf, 0, 0)
                k.op('dve', 'tensor_tensor', [v3(hf[:, :])], [v3(hf[:, :]), bc16(pf, pf.h[:, :]), ALU.mult])
                k.op('dve', 'scalar_tensor_tensor', [facc[:]], [facc[:], mab[:, 0:1], hf[:]], op0=ALU.mult, op1=ALU.add)
                k.op('dve', 'scalar_tensor_tensor', [facc[:]], [hb[:], mab[:, 1:2], facc[:]], op0=ALU.mult, op1=ALU.add)
                for q in range(4):
                    o = exchange(facc[:, q * 256:(q + 1) * 256], 256)
                    k.op('dve', 'tensor_copy', [facc[:, q * 256:(q + 1) * 256]], [o[:, 0:256]])
                load_state(hf, 0, 0)
                k.op('dve', 'scalar_tensor_tensor', [hf[:]], [facc[:], mab[:, 1:2], hf[:]], op0=ALU.mult, op1=ALU.add)
                k.op('dve', 'tensor_scalar', [facc[:]], [facc[:], mab[:, 0:1], None, ALU.mult])
            if KSTOP <= 3:
                continue
            for c in range(nch):
                gc = gc0 + c
                tg = gc * 128
                prep(gc)
                k.op('act', 'copy', [hpf], [hf[:]])
                k.dma('sp', hpb, hpb_d[c, :, :])
                k.dma('sp', zsv, z_d[tg:tg + 128, :])
                if sample:
                    for g in range(2):
                        t = fr()
                        k.op('dve', 'tensor_tensor', [v3(t[:, :])], [v3(facc[:, g * 512:(g + 1) * 512]), bc8(pbs, pbs.h[:, c, g * 8:(g + 1) * 8]), ALU.mult])
                        k.op('dve', 'tensor_tensor', [V(hpb[0], hpb[1][:, g * 512:(g + 1) * 512])], [V(hpb[0], hpb[1][:, g * 512:(g + 1) * 512]), t[:], ALU.add])
                k.op('dve', 'tensor_tensor', [v3(xdf)], [v3(xs_t), bc16(dts, dts.h[:, gc, 0:16]), ALU.mult])
                k.op('dve', 'tensor_tensor', [v3(xdb)], [v3(xs_t), bc16(dts, dts.h[:, gc, 16:32]), ALU.mult])
                for g in range(2):
                    p = ps()
                    k.op('pe', 'matmul', [p[:, 0:128]], [V(A1, xbv[:, 8 + g, :]), V(A1, xbv[:, 10 + g, :])], start=True, stop=True)
                    k.op('act', 'copy', [sc_t[:, g * 128:(g + 1) * 128]], [p[:, 0:128]])
                if KSTOP <= 4:
                    continue
                k.op('dve', 'tensor_tensor', [v3(xdd)], [v3(xs_t), bc16(rows, rows.h[:, 2, 0:16]), ALU.mult])
                for h_ in range(NH):
                    g = h_ // 8
                    yslice = YP[g][:, (h_ % 8) * 64:(h_ % 8 + 1) * 64]
                    for d in range(2):
                        dh = fr()
                        k.op('dve', 'tensor_scalar', [dh[:, 0:128]], [onesf[:, :], sA[:, d * 16 + h_:d * 16 + h_ + 1], None, ALU.mult])
                        p = ps()
                        k.op('pe', 'matmul', [p[:, 0:128]], [dh[:, 0:128], cst[:, 1 + d, :]], start=True, stop=True)
                        sg_ = fr()
                        k.op('dve', 'scalar_tensor_tensor', [sg_[:, 0:128]],
                             [p[:, 0:128], acs[:, gc, d * 16 + h_:d * 16 + h_ + 1], cst[:, 3 + d, :]], op0=ALU.subtract, op1=ALU.add)
                        k.op('act', 'activation', [sg_[:, 0:128]], [sg_[:, 0:128], AF.Exp])
                        G = br()
                        k.op('dve', 'tensor_tensor', [G[:, 0:128]], [sg_[:, 0:128], sc_t[:, g * 128:(g + 1) * 128], ALU.mult])
                        et = fr()
                        k.op('act', 'activation', [et[:, 0:128]], [p[:, 0:128], AF.Exp])
                        k.op('dve', 'tensor_tensor', [G[:, 128:256]], [et[:, 0:128], V(A1, xbv[:, 10 + g, :]), ALU.mult])
                        xd_ = xdf if d == 0 else xdb
                        hp_ = hpf if d == 0 else hpb
                        k.op('pe', 'matmul', [yslice], [G[:, 0:128], V(xd_[0], xd_[1][:, h_ * 64:(h_ + 1) * 64])], start=(d == 0), stop=False)
                        k.op('pe', 'matmul', [yslice], [G[:, 128:256], V(hp_[0], hp_[1][:, h_ * 64:(h_ + 1) * 64])], start=False, stop=False)
                    k.op('pe', 'matmul', [yslice], [identb[:], V(xdd[0], xdd[1][:, h_ * 64:(h_ + 1) * 64])], start=False, stop=True)
                if KSTOP <= 6:
                    continue
                state_upd(hf, 0)
                for g in range(2):
                    k.op('dve', 'tensor_tensor', [ya[g][:]], [V(zsv[0], zsv[1][:, g * 512:(g + 1) * 512]), YP[g][:], ALU.mult])
                    t = fr()
                    k.op('act', 'activation', [t[:]], [ya[g][:], AF.Square])
                    k.op('dve', 'reduce_sum', [ssq[:, g:g + 1]], [t[:]], axis=mybir.AxisListType.X)
                k.op('dve', 'tensor_tensor', [ssq[:, 2:3]], [ssq[:, 0:1], ssq[:, 1:2], ALU.add])
                k.op('act', 'activation', [ssq[:, 2:3]], [ssq[:, 2:3], AF.Ln], bias=EPS, scale=1.0 / D)
                k.op('act', 'activation', [ssq[:, 2:3]], [ssq[:, 2:3], AF.Exp], scale=-0.5)
                yab = br()
                for g in range(2):
                    k.op('dve', 'tensor_scalar', [yab[:, g * 512:(g + 1) * 512]], [ya[g][:], ssq[:, 2:3], None, ALU.mult])
                p = pb()
                for cc in range(8):
                    k.op('pe', 'transpose', [p[:, cc * 128:(cc + 1) * 128]], [yab[:, cc * 128:(cc + 1) * 128], identb[:]])
                for cc in range(8):
                    k.op('act', 'activation', [V(A2, ycv[:, cc, :])], [p[:, cc * 128:(cc + 1) * 128], AF.Identity], scale=vA(RA_SG + j, cc))
                if KSTOP <= 7:
                    continue
                col0 = s_i * SW + c * 128
                k.dma('sp', V(A2, pinv), raw_d[:, 12:16, col0:col0 + 144])
                ic = fr()
                icv = ic.h[:, :].rearrange("p (g w) -> p g w", g=4)
                k.dma('sp', V(ic, icv), (ics_d if sample else icp_d)[:, :, (c * 128):(c * 128) + 128])
                pob = br()
                pobv = pob.h[:, 0:512].rearrange("p (g w) -> p g w", g=4)
                for gi, win_ in enumerate((2, 4, 8, 16)):
                    S_ = fr()
                    U = S_[:, 0:144]
                    k.op('dve', 'tensor_copy', [U], [V(A2, pinv[:, gi, :])])
                    src = (S_, 0)
                    w_ = 1; cur = 0; n_ = 144
                    regs = [144, 288]
                    while w_ < win_:
                        n2 = n_ - w_
                        dst = regs[0] if cur != regs[0] else regs[1]
                        k.op('dve', 'tensor_tensor', [S_[:, dst:dst + n2]], [S_[:, cur:cur + n2], S_[:, cur + w_:cur + w_ + n2], ALU.add])
                        cur = dst; n_ = n2; w_ *= 2
                    st_ = 8 - win_ // 2
                    k.op('dve', 'tensor_tensor', [S_[:, 432:432 + 0] if False else S_[:, cur + st_:cur + st_ + 128]],
                         [S_[:, cur + st_:cur + st_ + 128], V(ic, icv[:, gi, :]), ALU.mult])
                    k.op('dve', 'tensor_tensor', [V(pob, pobv[:, gi, :])], [S_[:, cur + st_:cur + st_ + 128], S_[:, 8:136], ALU.subtract])
                for gi in range(4):
                    p = ps()
                    k.op('pe', 'matmul', [p[:, 0:128]], [pw[:, gi * 128:(gi + 1) * 128], V(pob, pobv[:, gi, :])], start=True, stop=True)
                    k.op('act', 'activation', [V(A2, ycv[:, 8 + gi, :])], [p[:, 0:128], AF.Identity], scale=vA(RA_PS + j, gi))
                if KSTOP <= 8:
                    continue
                for oc in range(8):
                    p = ps()
                    for kc in range(12):
                        wq = WO[kc // 4]
                        wqv = wq.h[:, :].rearrange("p (kc n) -> p kc n", kc=4)
                        k.op('pe', 'matmul', [p[:, 0:128]], [V(wq, wqv[:, kc % 4, oc * 128:(oc + 1) * 128]), V(A2, ycv[:, kc, :])],
                             start=(kc == 0), stop=(kc == 11))
                    k.op('dve', 'scalar_tensor_tensor', [x[:, oc, tg:tg + 128]],
                         [p[:, 0:128], sm[:, oc, 3:4], x[:, oc, tg:tg + 128]], op0=ALU.mult, op1=ALU.add)
            if not sample:
                store_state(hf, seq0 + s_i, 0)

    def odd_mixer(i, j, cond, L, nseg, sample):
        TN = L * nseg; SW = L + 32
        win = od_w_in[j].rearrange("(kc p) n -> p kc n", p=128)
        wov = od_w_out[j].rearrange("(kc p) n -> p kc n", p=128)
        k.op('dve', 'tensor_copy', [sm[:, :, 3]], [mvec(i, 5, cond)])
        stg = xr()
        k.dma('sp', V(stg, stg.h[:, :].rearrange("p (h s) -> p h s", h=8)), od_sp_w[j].rearrange("h t s -> t h s"))
        for h_ in range(8):
            p = ps()
            k.op('pe', 'transpose', [p[:, 0:128]], [stg[:, h_ * 128:(h_ + 1) * 128], cst[:, 0, :]])
            k.op('act', 'copy', [spw[:, h_, :]], [p[:, 0:128]])
        k.dma('sp', V(spbb, spbb.h[:, :].unsqueeze(1)), spb_d[j, 0:1, :].partition_broadcast(128))
        for b_ in range(2):
            wa = wr(); wb = wr()
            wav = wa.h[:, :].rearrange("p (kc n) -> p kc n", kc=8)
            wbv = wb.h[:, :].rearrange("p (kc n) -> p kc n", kc=8)
            k.dma('pool', V(wa, wav), win[:, :, 2048 + b_ * 512:2048 + (b_ + 1) * 512])
            k.dma('pool', V(wb, wbv), win[:, :, 3072 + b_ * 512:3072 + (b_ + 1) * 512])
            for t0 in range(0, TN, 512):
                for q in range(4):
                    pa = ps(); pg = ps()
                    for kc in range(8):
                        k.op('pe', 'matmul', [pa[:]], [V(wa, wav[:, kc, q * 128:(q + 1) * 128]), hn[:, kc, t0:t0 + 512]], start=(kc == 0), stop=(kc == 7))
                    for kc in range(8):
                        k.op('pe', 'matmul', [pg[:]], [V(wb, wbv[:, kc, q * 128:(q + 1) * 128]), hn[:, kc, t0:t0 + 512]], start=(kc == 0), stop=(kc == 7))
                    sg_ = fr()
                    k.op('act', 'activation', [sg_[:]], [pg[:], AF.Sigmoid])
                    st = br()
                    k.op('dve', 'tensor_tensor', [st[:, 0:512]], [sg_[:], pa[:], ALU.mult])
                    store_padded(glu_d, b_ * 4 + q, st[:, 0:512], t0, L, 16)
        if sample:
            halo_exchange(glu_d, 8, 16, L)
        else:
            zero_pads(glu_d, 8, 16, L, nseg)
        SA, SB, SC, SD = WR[0], WR[1], WR[2], WR[3]
        sav = SA.h[:, :].rearrange("p (c n) -> p c n", c=8)
        sbv = SB.h[:, :].rearrange("p (c n) -> p c n", c=8)
        scv = SC.h[:, :].rearrange("p (c n) -> p c n", c=8)
        sdv = SD.h[:, :].rearrange("p (c n) -> p c n", c=8)
        wslots = [WR[4], WR[5]]
        wctr = [0]

        def wslot():
            w_ = wslots[wctr[0] % 2]; wctr[0] += 1
            return w_

        mean_t, rs_t = ya[0], ya[1]

        def ln_stats(S_, sv_):
            p1 = ps()
            for c in range(8):
                k.op('pe', 'matmul', [p1[:]], [onesb[:], V(S_, sv_[:, c, :])], start=(c == 0), stop=(c == 7))
            for c in range(8):
                k.op('act', 'activation', [V(SC, scv[:, c, :])], [V(S_, sv_[:, c, :]), AF.Square])
            p2 = ps()
            for c in range(8):
                k.op('pe', 'matmul', [p2[:]], [onesb[:], V(SC, scv[:, c, :])], start=(c == 0), stop=(c == 7))
            k.op('dve', 'tensor_scalar', [mean_t[:]], [p1[:], 1.0 / D, None, ALU.mult])
            t = fr()
            k.op('dve', 'tensor_tensor', [t[:]], [mean_t[:], mean_t[:], ALU.mult])
            k.op('dve', 'scalar_tensor_tensor', [rs_t[:]], [p2[:], 1.0 / D, t[:]], op0=ALU.mult, op1=ALU.subtract)
            k.op('act', 'activation', [rs_t[:]], [rs_t[:], AF.Ln], bias=EPS, scale=1.0)
            k.op('act', 'activation', [rs_t[:]], [rs_t[:], AF.Exp], scale=-0.5)

        def ln_apply(S_, sv_, func, grow, brow, bank):
            for c in range(8):
                t = fr()
                k.op('dve', 'tensor_tensor', [t[:]], [V(S_, sv_[:, c, :]), mean_t[:], ALU.subtract])
                k.op('dve', 'tensor_tensor', [t[:]], [t[:], rs_t[:], ALU.mult])
                k.op('act', 'activation', [V(S_, sv_[:, c, :])], [t[:], func], scale=bank(grow, c), bias=bank(brow, c))

        for t0 in range(0, TN, 512):
            for (S_, sv_, colbase) in ((SA, sav, 0), (SB, sbv, 1024)):
                for blk in range(2):
                    w_ = wslot()
                    wv_ = w_.h[:, :].rearrange("p (kc n) -> p kc n", kc=8)
                    k.dma('pool', V(w_, wv_), win[:, :, colbase + blk * 512:colbase + (blk + 1) * 512])
                    for q in range(4):
                        p = ps()
                        for kc in range(8):
                            k.op('pe', 'matmul', [p[:]], [V(w_, wv_[:, kc, q * 128:(q + 1) * 128]), hn[:, kc, t0:t0 + 512]], start=(kc == 0), stop=(kc == 7))
                        k.op('act', 'activation', [V(S_, sv_[:, blk * 4 + q, :])], [p[:], AF.Gelu_apprx_tanh])
            ln_stats(SB, sbv)
            ln_apply(SB, sbv, AF.Identity, RA_VG + j, RA_VB + j, vA)
            for cq in range(4):
                cols = slice(cq * 128, (cq + 1) * 128)
                p = pb()
                for h_ in range(8):
                    k.op('pe', 'transpose', [p[:, h_ * 128:(h_ + 1) * 128]], [V(SB, sbv[:, h_, cols]), identb[:]])
                vt = br()
                k.op('act', 'copy', [vt[:]], [p[:]])
                for h_ in range(8):
                    p2 = ps()
                    k.op('pe', 'matmul', [p2[:, 0:128]], [vt[:, h_ * 128:(h_ + 1) * 128], spw[:, h_, :]], start=True, stop=True)
                    t = fr()
                    k.op('dve', 'tensor_tensor', [t[:, 0:128]], [p2[:, 0:128], spbb[:, h_ * 128:(h_ + 1) * 128], ALU.add])
                    k.op('dve', 'tensor_tensor', [V(SA, sav[:, h_, cols])], [V(SA, sav[:, h_, cols]), t[:, 0:128], ALU.mult])
            tw = min(512, L)
            for sub in range(512 // tw):
                tt = t0 + sub * tw
                s_i = tt // L; lt = tt % L
                for half in range(2):
                    gv = SC.h[:, 0:4 * (tw + 30)].rearrange("p (c w) -> p c w", c=4)
                    col0 = s_i * SW + 16 + lt - 15
                    k.dma('sp', V(SC, gv), glu_d[:, half * 4:(half + 1) * 4, col0:col0 + tw + 30])
                    for c4 in range(4):
                        c = half * 4 + c4
                        acc = fr()
                        k.op('dve', 'tensor_scalar', [acc[:, 0:tw]], [V(SC, gv[:, c4, 0:tw]), vB(RB_DW + j * 31, c), None, ALU.mult])
                        for kk in range(1, 31):
                            k.op('dve', 'scalar_tensor_tensor', [acc[:, 0:tw]],
                                 [V(SC, gv[:, c4, kk:kk + tw]), vB(RB_DW + j * 31 + kk, c), acc[:, 0:tw]], op0=ALU.mult, op1=ALU.add)
                        k.op('act', 'activation', [V(SD, sdv[:, c, sub * tw:(sub + 1) * tw])], [acc[:, 0:tw], AF.Identity],
                             bias=vB(RB_DB + j, c), scale=1.0)
            ln_stats(SD, sdv)
            ln_apply(SD, sdv, AF.Silu, RB_CG + j, RB_CBB + j, vB)
            for ob in range(4):
                w_ = wslot()
                wv_ = w_.h[:, :].rearrange("p (kc n) -> p kc n", kc=16)
                k.dma('pool', V(w_, wv_), wov[:, :, ob * 256:(ob + 1) * 256])
                for o2 in range(2):
                    oc = ob * 2 + o2
                    p = ps()
                    for kc in range(16):
                        rhs = V(SA, sav[:, kc, :]) if kc < 8 else V(SD, sdv[:, kc - 8, :])
                        k.op('pe', 'matmul', [p[:]], [V(w_, wv_[:, kc, o2 * 128:(o2 + 1) * 128]), rhs], start=(kc == 0), stop=(kc == 15))
                    k.op('dve', 'scalar_tensor_tensor', [x[:, oc, t0:t0 + 512]],
                         [p[:], sm[:, oc, 3:4], x[:, oc, t0:t0 + 512]], op0=ALU.mult, op1=ALU.add)

    for (src_d, dst_d, TN, addpos, cond) in ((xp_d, yp_d, TP, False, 0), (xs_d, ys_d, TS, True, 1)):
        sample = (cond == 1)
        if ('S' if sample else 'P') not in KPASS:
            continue
        L, nseg = (TS, 1) if sample else (256, 4)
        load_x(src_d, TN, addpos)
        for i in range(depth):
            normmod(i, 0, cond, TN)
            ffn(i, 0, cond, TN)
            normmod(i, 1, cond, TN)
            if i % 2 == 0:
                even_mixer(i, i // 2, cond, L, nseg, sample, 0)
            else:
                odd_mixer(i, i // 2, cond, L, nseg, sample)
            normmod(i, 2, cond, TN)
            ffn(i, 1, cond, TN)
        final_store(dst_d, TN)
    k.finish('sp')
    env = dict(locals())
    return env


def _host_inputs(inp, c):
    f = np.float32
    b = c // 2; half = c % 2
    d = {}
    d["xp"] = np.ascontiguousarray(inp["x_prompt"][4 * c:4 * c + 4].reshape(TP, D)).astype(f)
    d["xs"] = np.ascontiguousarray(inp["x_sample"][b, half * TS:(half + 1) * TS]).astype(f)
    st = inp["state_ssd"][b]
    d["st"] = np.ascontiguousarray(st.reshape(2, 2, NH * HP, NST)).astype(f)
    rowsA = []
    rowsA += [inp["norm_g"][i, kk] for i in range(DEPTH) for kk in range(3)]
    rowsA += [inp["final_norm_g"]]
    rowsA += [inp["b_mod"][i].reshape(9, D)[kk] for i in range(DEPTH) for kk in range(9)]
    def two(v):
        o = np.zeros((2, D), f); o.reshape(-1)[:v.shape[0]] = v; return [o[0], o[1]]
    for j in range(2):
        for kk in range(5):
            rowsA += two(inp["ev_conv_w"][j, kk])
    for j in range(2):
        rowsA += two(inp["ev_conv_b"][j])
    for j in range(2):
        o = np.zeros((D,), f); o[:512] = inp["ev_pool_scale"][j]; rowsA.append(o)
    rowsA += [inp["c_ctx"], inp["c"][b]]
    rowsA += [inp["ev_ssd_norm_g"][j] for j in range(2)] + [inp["od_v_ln_g"][j] for j in range(2)] + [inp["od_v_ln_b"][j] for j in range(2)]
    d["bankA"] = np.stack(rowsA).astype(f)
    assert d["bankA"].shape == (NRA, D)
    rowsB = [inp["od_dw_w"][j, kk] for j in range(2) for kk in range(31)]
    rowsB += [inp["od_dw_b"][j] for j in range(2)] + [inp["od_cn_g"][j] for j in range(2)] + [inp["od_cn_b"][j] for j in range(2)]
    d["bankB"] = np.stack(rowsB).astype(f)
    i_ = np.arange(128)
    cst = np.zeros((128, 5, 128), f)
    cst[:, 0] = np.eye(128)
    cst[:, 1] = (i_[:, None] <= i_[None, :])
    cst[:, 2] = (i_[:, None] >= i_[None, :])
    cst[:, 3] = np.where(i_[:, None] <= i_[None, :], 0.0, NEG)
    cst[:, 4] = np.where(i_[:, None] >= i_[None, :], 0.0, NEG)
    d["cst"] = cst
    d["eye16"] = np.eye(16, dtype=f)
    mab = np.zeros((128, 2), f); mab[:, half] = 1.0
    d["mab"] = mab
    def invcnt(tglob, Ltot):
        o = np.zeros((4, tglob.shape[0]), f)
        for gi, w_ in enumerate((2, 4, 8, 16)):
            lo = np.clip(tglob - w_ // 2, 0, Ltot); hi = np.clip(tglob - w_ // 2 + w_, 0, Ltot)
            o[gi] = 1.0 / (hi - lo).astype(f)
        return np.ascontiguousarray(np.broadcast_to(o[None], (128,) + o.shape)).astype(f)
    d["icp"] = invcnt(np.arange(256), 256)
    d["ics"] = invcnt(half * TS + np.arange(TS), 2 * TS)
    quarter = D // 4
    freqs = np.exp(-math.log(10000.0) * np.arange(quarter, dtype=f) / quarter).astype(f)
    tg = half * TS + np.arange(TS)
    row = (tg // 64).astype(f); col = (tg % 64).astype(f)
    ar = row[:, None] * freqs; ac = col[:, None] * freqs
    d["pos"] = np.concatenate([np.sin(ar), np.cos(ar), np.sin(ac), np.cos(ac)], -1).astype(f)
    rws = np.zeros((2, 3, 32), f)
    for j in range(2):
        rws[j, 0] = inp["ev_dt_bias"][j].reshape(32); rws[j, 1] = inp["ev_a_log"][j].reshape(32)
        rws[j, 2, :16] = inp["ev_d_skip"][j]
    d["rows"] = rws
    d["spb"] = np.ascontiguousarray(inp["od_sp_b"].reshape(2, 1, D).repeat(2, 1)).astype(f)
    for name in ("w_mod", "ffn_w_gate", "ffn_w_up", "ffn_w_down", "ev_w_in", "ev_pool_w", "ev_w_out",
                 "od_w_in", "od_sp_w", "od_w_out"):
        d[name] = np.ascontiguousarray(inp[name]).astype(f)
    return d


_CACHE = {}


def kernel(**inputs):
    inputs = {k_: np.asarray(v) for k_, v in inputs.items()}
    if "nc" not in _CACHE:
        _CACHE["nc"] = build()["nc"]
    nc = _CACHE["nc"]
    in_maps = [_host_inputs(inputs, c) for c in range(8)]
    res = run_bass_kernel_spmd(nc, in_maps, core_ids=list(range(8)))
    yp = np.stack([res.results[c]["yp"] for c in range(8)]).reshape(32, 256, D).astype(np.float32)
    ys = np.stack([res.results[c]["ys"] for c in range(8)]).reshape(4, 4096, D).astype(np.float32)
    ns = np.stack([res.results[c]["ns"] for c in range(8)]).reshape(32, 2, 2, NH, HP, NST).astype(np.float32)
    return yp, ys, ns
```
